# Optimizing a Trainium2 kernel written in Bass

```python
import jax, jax.numpy as jnp
from jax import lax
import numpy as np


D_MODEL = 4096
BATCH = 1
SEQ = 16384
DEPTH = 1
DEC_BATCH = 8
DEC_SEQ = 2048
PAST_LEN = 128

HEAD_DIM = 128
A_HEADS = 16
A_WIDTH = A_HEADS * HEAD_DIM
CHUNK = 128
B_HEADS = 16
B_WIDTH = B_HEADS * HEAD_DIM
DILATION_PATTERNS = ((128, 1), (512, 4), (2048, 16))
MIX_WIDTH = A_WIDTH + B_WIDTH
IN_WIDTH = 2 * A_WIDTH + 3 * B_WIDTH
N_MEM = 256
X_HEADS = 4
X_WIDTH = X_HEADS * HEAD_DIM
D_FF = 11008
CONV_WIDTH = 3
EPS = 1e-6
NEG = -1e30

kernel_name = 'hybrid_sgmlp_dilated_attn_encoder'


def rmsnorm(x, g):
    xf = x.astype(jnp.float32)
    y = xf * lax.rsqrt(jnp.mean(xf * xf, axis=-1, keepdims=True) + EPS)
    return (y * g.astype(jnp.float32)).astype(x.dtype)


def layernorm(x, g, b):
    xf = x.astype(jnp.float32)
    mu = jnp.mean(xf, axis=-1, keepdims=True)
    var = jnp.mean(jnp.square(xf - mu), axis=-1, keepdims=True)
    y = (xf - mu) * lax.rsqrt(var + EPS)
    return (y * g.astype(jnp.float32) + b.astype(jnp.float32)).astype(x.dtype)


def alibi_slopes(n_heads):
    return jnp.exp2(-8.0 * jnp.arange(1, n_heads + 1, dtype=jnp.float32) / n_heads)


def spatial_gating(z, ln_g, ln_b, w_s, b_s):
    u, v = jnp.split(z, 2, axis=-1)
    v = layernorm(v, ln_g, ln_b)
    B, S, _ = v.shape
    v = v.reshape(B, S // CHUNK, CHUNK, A_HEADS, HEAD_DIM)
    mixed = jnp.einsum('gts,bcsge->bctge', w_s, v) + b_s.T[None, None, :, :, None]
    return u * mixed.reshape(B, S, A_WIDTH)


def dilated_window_attention(q, k, v, slopes, window, dilation):
    B, S, H, E = q.shape
    half = (window // 2) // dilation
    blk = half
    unit = dilation * blk
    Sp = -(-S // unit) * unit
    nb = Sp // unit
    pad = ((0, 0), (0, Sp - S), (0, 0), (0, 0))

    def to_blocks(a):
        return jnp.pad(a, pad).reshape(B, nb, blk, dilation, H, E)

    def with_neighbours(a):
        ap = jnp.pad(a, ((0, 0), (1, 1)) + ((0, 0),) * (a.ndim - 2))
        return jnp.concatenate([ap[:, :-2], ap[:, 1:-1], ap[:, 2:]], axis=2)

    qb = to_blocks(q).astype(jnp.float32)
    kb = with_neighbours(to_blocks(k)).astype(jnp.float32)
    vb = with_neighbours(to_blocks(v)).astype(jnp.float32)
    valid = (jnp.arange(Sp) < S).reshape(1, nb, blk, dilation)
    kvalid = with_neighbours(valid)

    s = jnp.einsum('bnirhe,bnjrhe->bnrhij', qb, kb) * (HEAD_DIM ** -0.5)
    dist = jnp.abs(jnp.arange(3 * blk)[None, :] - blk - jnp.arange(blk)[:, None])
    bias = -slopes[:, None, None] * (dist * dilation).astype(jnp.float32)
    mask = (dist <= half)[None, None, None, None] & kvalid.transpose(0, 1, 3, 2)[:, :, :, None, None, :]
    s = jnp.where(mask, s + bias, NEG)
    m = jnp.max(s, axis=-1, keepdims=True)
    p = jnp.exp(s - m)
    den = jnp.sum(p, axis=-1, keepdims=True)
    o = jnp.einsum('bnrhij,bnjrhe->bnirhe', p / den, vb)
    lse = (m + jnp.log(den))[..., 0]
    o = o.reshape(B, Sp, H, E)[:, :S]
    lse = lse.transpose(0, 1, 4, 2, 3).reshape(B, Sp, H)[:, :S]
    return o, lse


def dilated_attention_mixture(q, k, v):
    slopes = alibi_slopes(B_HEADS)
    outs, lses = [], []
    for window, dilation in DILATION_PATTERNS:
        o, lse = dilated_window_attention(q, k, v, slopes, window, dilation)
        outs.append(o)
        lses.append(lse)
    w = jax.nn.softmax(jnp.stack(lses, axis=0), axis=0)
    o = jnp.einsum('pbsh,pbshe->bshe', w, jnp.stack(outs, axis=0))
    return o.astype(q.dtype)


def memory_cross_attention(h, mem, mem_g, w_xq, w_xkv, w_xo):
    B, S, _ = h.shape
    q = (h @ w_xq).reshape(B, S, X_HEADS, HEAD_DIM)
    kv = (rmsnorm(mem, mem_g) @ w_xkv).reshape(B, mem.shape[1], 2, X_HEADS, HEAD_DIM)
    k, v = kv[:, :, 0], kv[:, :, 1]
    s = jnp.einsum('bshe,bnhe->bhsn', q.astype(jnp.float32), k.astype(jnp.float32)) * (HEAD_DIM ** -0.5)
    p = jax.nn.softmax(s, axis=-1)
    o = jnp.einsum('bhsn,bnhe->bshe', p, v.astype(jnp.float32)).astype(h.dtype)
    return o.reshape(B, S, X_WIDTH) @ w_xo


def conv_gated_ffn(h, w_up, conv_w, conv_b, w_down):
    S = h.shape[1]
    z = h @ w_up
    r = CONV_WIDTH // 2
    zp = jnp.pad(z, ((0, 0), (r, r), (0, 0)))
    z = sum(zp[:, i:i + S] * conv_w[i] for i in range(CONV_WIDTH)) + conv_b
    gate, val = jnp.split(z, 2, axis=-1)
    return (jax.nn.silu(gate) * val) @ w_down


def encoder_layer(x, mem, norm_mix_g, w_in, sg_ln_g, sg_ln_b, sg_w, sg_b, grp_a_g, grp_b_g, w_out,
                  norm_x_g, mem_norm_g, w_xq, w_xkv, w_xo, norm_ffn_g, w_up, conv_w, conv_b, w_down):
    B, S, _ = x.shape
    h = rmsnorm(x, norm_mix_g)
    proj = h @ w_in
    za = proj[..., :2 * A_WIDTH]
    qkv = proj[..., 2 * A_WIDTH:].reshape(B, S, 3, B_HEADS, HEAD_DIM)
    a_out = spatial_gating(jax.nn.gelu(za), sg_ln_g, sg_ln_b, sg_w, sg_b)
    b_out = dilated_attention_mixture(qkv[:, :, 0], qkv[:, :, 1], qkv[:, :, 2]).reshape(B, S, B_WIDTH)
    mixed = jnp.concatenate([rmsnorm(a_out, grp_a_g), rmsnorm(b_out, grp_b_g)], axis=-1)
    x = x + mixed @ w_out
    x = x + memory_cross_attention(rmsnorm(x, norm_x_g), mem, mem_norm_g, w_xq, w_xkv, w_xo)
    x = x + conv_gated_ffn(rmsnorm(x, norm_ffn_g), w_up, conv_w, conv_b, w_down)
    return x


def setup_inputs(seed: int = 0) -> dict:
    key = jax.random.key(seed)
    ks = jax.random.split(key, 24)
    f32 = jnp.float32

    def nrm(k, shape, scale):
        return jax.random.normal(k, shape, f32) * scale

    def gain(k, shape):
        return 1.0 + 0.01 * jax.random.normal(k, shape, f32)

    L = DEPTH
    return {
        'x_prompt': nrm(ks[0], (BATCH, SEQ, D_MODEL), 1.0),
        'x_sample': nrm(ks[1], (DEC_BATCH, DEC_SEQ, D_MODEL), 1.0),
        'mem_prompt': nrm(ks[2], (BATCH, N_MEM, D_MODEL), 1.0),
        'mem_sample': nrm(ks[3], (DEC_BATCH, N_MEM, D_MODEL), 1.0),
        'norm_mix_g': gain(ks[4], (L, D_MODEL)),
        'w_in': nrm(ks[5], (L, D_MODEL, IN_WIDTH), D_MODEL ** -0.5),
        'sg_ln_g': gain(ks[6], (L, A_WIDTH)),
        'sg_ln_b': nrm(ks[7], (L, A_WIDTH), 0.01),
        'sg_w': nrm(ks[8], (L, A_HEADS, CHUNK, CHUNK), CHUNK ** -0.5),
        'sg_b': gain(ks[9], (L, A_HEADS, CHUNK)),
        'grp_a_g': gain(ks[10], (L, A_WIDTH)),
        'grp_b_g': gain(ks[11], (L, B_WIDTH)),
        'w_out': nrm(ks[12], (L, MIX_WIDTH, D_MODEL), MIX_WIDTH ** -0.5),
        'norm_x_g': gain(ks[13], (L, D_MODEL)),
        'mem_norm_g': gain(ks[14], (L, D_MODEL)),
        'w_xq': nrm(ks[15], (L, D_MODEL, X_WIDTH), D_MODEL ** -0.5),
        'w_xkv': nrm(ks[16], (L, D_MODEL, 2 * X_WIDTH), D_MODEL ** -0.5),
        'w_xo': nrm(ks[17], (L, X_WIDTH, D_MODEL), X_WIDTH ** -0.5),
        'norm_ffn_g': gain(ks[18], (L, D_MODEL)),
        'w_up': nrm(ks[19], (L, D_MODEL, 2 * D_FF), D_MODEL ** -0.5),
        'conv_w': nrm(ks[20], (L, CONV_WIDTH, 2 * D_FF), CONV_WIDTH ** -0.5),
        'conv_b': nrm(ks[21], (L, 2 * D_FF), 0.01),
        'w_down': nrm(ks[22], (L, D_FF, D_MODEL), D_FF ** -0.5),
        'final_g': gain(ks[23], (D_MODEL,)),
    }


def reference(x_prompt, x_sample, mem_prompt, mem_sample, norm_mix_g, w_in, sg_ln_g, sg_ln_b, sg_w, sg_b,
              grp_a_g, grp_b_g, w_out, norm_x_g, mem_norm_g, w_xq, w_xkv, w_xo, norm_ffn_g, w_up,
              conv_w, conv_b, w_down, final_g):
    layer_params = (norm_mix_g, w_in, sg_ln_g, sg_ln_b, sg_w, sg_b, grp_a_g, grp_b_g, w_out,
                    norm_x_g, mem_norm_g, w_xq, w_xkv, w_xo, norm_ffn_g, w_up, conv_w, conv_b, w_down)

    def run(x, mem):
        for l in range(DEPTH):
            x = encoder_layer(x, mem, *[p[l] for p in layer_params])
        return rmsnorm(x, final_g)

    y_prompt = run(x_prompt, mem_prompt)
    y_sample = run(x_sample, mem_sample)
    return (y_prompt, y_sample)
```

```python
import contextlib
import math
import numpy as np
import concourse.bass as bass
import concourse.mybir as mybir
from concourse.bass_utils import run_bass_kernel_spmd

F32 = mybir.dt.float32
BF16 = mybir.dt.bfloat16
AF = mybir.ActivationFunctionType
ALU = mybir.AluOpType

SAME_ENGINE_SYNC = True
EPS = 1e-6
NEGBIG = 30000.0
DEN_MERGE = True


class _Op:
    __slots__ = ("eng", "fn", "deps", "is_dma", "dma_key", "sig", "val", "idx")

    def __init__(self, eng, fn, is_dma, dma_key):
        self.eng = eng
        self.fn = fn
        self.deps = []
        self.is_dma = is_dma
        self.dma_key = dma_key
        self.sig = False
        self.val = 0
        self.idx = 0


class Prog:
    ENGS = ("pe", "act", "dve", "pool", "sp")

    def __init__(self, nc):
        self.nc = nc
        self.ops = {e: [] for e in self.ENGS}
        self.last_w = {}
        self.readers = {}
        self.n_ops = 0
        self.last_eng = {}
        self.last_dma = {}

    def op(self, eng, fn, reads=(), writes=(), dma=None, extra_deps=()):
        o = _Op(eng, fn, dma is not None, dma)
        o.idx = self.n_ops
        self.n_ops += 1
        deps = {}
        for t in reads:
            w = self.last_w.get(t)
            if w is not None:
                deps[id(w)] = w
        for t in writes:
            w = self.last_w.get(t)
            if w is not None:
                deps[id(w)] = w
            rs = self.readers.get(t)
            if rs:
                for r in rs.values():
                    deps[id(r)] = r
        for d in extra_deps:
            deps[id(d)] = d
        o.deps = list(deps.values())
        for t in reads:
            rs = self.readers.setdefault(t, {})
            k = ("dma", o.dma_key) if o.is_dma else o.eng
            rs[k] = o
        for t in writes:
            self.last_w[t] = o
            self.readers[t] = {}
        self.ops[eng].append(o)
        if o.is_dma:
            self.last_dma[o.dma_key] = o
        else:
            self.last_eng[eng] = o
        return o

    def barrier(self):
        deps = list(self.last_eng.values()) + list(self.last_dma.values())
        for e in self.ENGS:
            self.op(e, lambda eng: eng.nop(), extra_deps=[d for d in deps])

    def emit(self, stack):
        nc = self.nc
        for e in self.ENGS:
            for o in self.ops[e]:
                for d in o.deps:
                    if d.is_dma:
                        d.sig = True
                    elif d.eng == o.eng and not o.is_dma:
                        if o.eng != "pe" and SAME_ENGINE_SYNC:
                            d.sig = True
                    else:
                        d.sig = True
        sems = {}
        cnt = {}
        for e in self.ENGS:
            for o in self.ops[e]:
                if o.is_dma:
                    k = ("dma", o.dma_key)
                    cnt[k] = cnt.get(k, 0) + 16
                    o.val = cnt[k]
                    o.sig = True
                elif o.sig:
                    k = ("eng", e)
                    cnt[k] = cnt.get(k, 0) + 1
                    o.val = cnt[k]
        final_dma = {k: v for k, v in cnt.items() if k[0] == "dma"}
        for i, k in enumerate(cnt):
            sems[k] = stack.enter_context(nc.semaphore("s%d" % i))
        engmap = {"pe": "tensor", "act": "scalar", "dve": "vector", "pool": "gpsimd", "sp": "sync"}
        block = stack.enter_context(nc.Block())
        prog = self

        def make(e):
            def body(eng):
                known = {}
                for o in prog.ops[e]:
                    need = {}
                    for d in o.deps:
                        if d.is_dma:
                            k = ("dma", d.dma_key)
                        else:
                            if d.eng == e and not o.is_dma:
                                if e == "pe" or not SAME_ENGINE_SYNC:
                                    continue
                            k = ("eng", d.eng)
                        if d.val > need.get(k, 0):
                            need[k] = d.val
                    for k, v in need.items():
                        if known.get(k, 0) >= v:
                            continue
                        known[k] = v
                        eng.wait_ge(sems[k], v)
                    ins = o.fn(eng)
                    if o.is_dma:
                        ins.then_inc(sems[("dma", o.dma_key)], 16)
                    elif o.sig:
                        ins.then_inc(sems[("eng", e)], 1)
                if e == "sp":
                    for k, v in final_dma.items():
                        if known.get(k, 0) < v:
                            eng.wait_ge(sems[k], v)
            return body

        for e in self.ENGS:
            getattr(block, engmap[e])(make(e))


class Cfg:
    def __init__(self, D=4096, AH=16, BH=16, XH=4, DFF=11008, FFG=8, TM=3, ARENA=28880):
        self.D, self.AH, self.BH, self.XH, self.DFF, self.FFG, self.TM = D, AH, BH, XH, DFF, FFG, TM
        self.KC = D // 128
        self.AW, self.BW, self.XW = AH * 128, BH * 128, XH * 128
        self.INW = 2 * self.AW + 3 * self.BW
        self.GF = DFF // 128
        self.NMEM = 256
        self.SEQ_P = 16384
        self.SEQ_S = 2048
        self.OWN = 2048
        self.MH = 128
        self.KH = 1024
        self.LM_P = self.OWN + 2 * self.MH
        self.LK_P = self.LM_P + 2 * self.KH
        self.LM_S = self.SEQ_S
        self.LK_S = self.SEQ_S
        self.ARENA = ARENA


DIL = (1, 4, 16)


class WT:
    def __init__(self, K, name, src, c0, c1, mode):
        self.K = K
        self.name = name
        self.KC = src.shape[0] // 128
        W = c1 - c0
        self.cw = min(256 if mode == "s" else 512, W)
        self.kcp = min(self.KC, 4096 // self.cw)
        self.ncb = W // self.cw
        self.nkp = -(-self.KC // self.kcp)
        self.c0 = c0
        self.dram = K.nc.dram_tensor("wd_" + name, [self.ncb, self.nkp, 128, self.kcp, self.cw], BF16).ap()
        self.srcv = src.rearrange("(kc p) n -> p kc n", p=128)

    def kcs(self, kp):
        return min(self.kcp, self.KC - kp * self.kcp)

    def precast_ops(self):
        return [(self, cb, kp) for cb in range(self.ncb) for kp in range(self.nkp)]

    def precast(self, cb, kp):
        K = self.K
        n = self.kcs(kp)
        dst = self.dram[cb, kp][:, 0:n, :]
        s = self.srcv[:, kp * self.kcp:kp * self.kcp + n, self.c0 + cb * self.cw: self.c0 + (cb + 1) * self.cw]
        key = K.next_cast_key()
        K.dma("pool", dst, s, [], [("wd", self.name, cb, kp), ("castkey", key)], key)

    def load(self, cb, kp):
        K = self.K
        i = K.wslot_next
        K.wslot_next = (i + 1) % K.NSLOT
        n = self.kcs(kp)
        view = K.wslots[i][:, 0:n * self.cw].rearrange("p (a b) -> p a b", b=self.cw)
        K.dma("sp", view, self.dram[cb, kp][:, 0:n, :], [("wd", self.name, cb, kp)], [("ws", i)], ("ws", i))
        return view, ("ws", i)


class Carve:
    def __init__(self, K):
        self.K = K
        self.off = 0

    def f32(self, shape):
        n = int(np.prod(shape[1:]))
        v = self.K.arena[:, self.off:self.off + n]
        self.off += n
        assert self.off <= self.K.arena_words, (self.off, self.K.arena_words)
        return self._shape(v, shape)

    def bf16(self, shape):
        n = int(np.prod(shape[1:]))
        w = (n + 1) // 2
        v = self.K.arena[:, self.off:self.off + w].bitcast(BF16)[:, 0:n]
        self.off += w
        assert self.off <= self.K.arena_words, (self.off, self.K.arena_words)
        return self._shape(v, shape)

    @staticmethod
    def _shape(v, shape):
        if len(shape) == 2:
            return v
        return v.rearrange("p (a b) -> p a b", b=shape[2])


class Kern:
    NSLOT = 6

    def __init__(self, cfg, arena_words=None):
        self.c = cfg
        self.nc = bass.Bass("TRN2", target_bir_lowering=False)
        self.P = Prog(self.nc)
        self.wslot_next = 0
        self.cast_i = 0
        self.rot = {}
        self.arena_words = arena_words or cfg.ARENA
        self.max_off = 0

    def next_cast_key(self):
        self.cast_i += 1
        return ("cast", self.cast_i % 8)

    def rr(self, name, n):
        i = self.rot.get(name, 0)
        self.rot[name] = (i + 1) % n
        return i

    def dma(self, q, out, in_, r, w, key):
        return self.P.op(q, lambda e: e.dma_start(out=out, in_=in_), reads=r, writes=w, dma=key)

    def mm(self, out, lhsT, rhs, start, stop, r, w, **kw):
        return self.P.op("pe", lambda e: e.matmul(out, lhsT=lhsT, rhs=rhs, start=start, stop=stop, **kw), reads=r, writes=w)

    def tr(self, out, in_, r, w):
        ident = self.ident[:]
        return self.P.op("pe", lambda e: e.transpose(out, in_, ident), reads=list(r) + ["ident"], writes=w)

    def act(self, out, in_, func, r, w, **kw):
        return self.P.op("act", lambda e: e.activation(out=out, in_=in_, func=func, **kw), reads=r, writes=w)

    def acopy(self, out, in_, r, w):
        return self.P.op("act", lambda e: e.copy(out=out, in_=in_), reads=r, writes=w)

    def vcopy(self, out, in_, r, w, eng="dve"):
        return self.P.op(eng, lambda e: e.tensor_copy(out=out, in_=in_), reads=r, writes=w)

    def tt(self, out, in0, in1, op, r, w, eng="dve"):
        return self.P.op(eng, lambda e: e.tensor_tensor(out=out, in0=in0, in1=in1, op=op), reads=r, writes=w)

    def ts(self, out, in0, s1, s2, op0, op1, r, w, eng="dve"):
        if s2 is None:
            return self.P.op(eng, lambda e: e.tensor_scalar(out=out, in0=in0, scalar1=s1, scalar2=None, op0=op0), reads=r, writes=w)
        return self.P.op(eng, lambda e: e.tensor_scalar(out=out, in0=in0, scalar1=s1, scalar2=s2, op0=op0, op1=op1), reads=r, writes=w)

    def stt(self, out, in0, scalar, in1, op0, op1, r, w, eng="dve"):
        return self.P.op(eng, lambda e: e.scalar_tensor_tensor(out=out, in0=in0, scalar=scalar, in1=in1, op0=op0, op1=op1), reads=r, writes=w)

    def rcp(self, out, in_, r, w):
        return self.P.op("dve", lambda e: e.reciprocal(out=out, in_=in_), reads=r, writes=w)

    def mset(self, out, val, w, eng="dve"):
        return self.P.op(eng, lambda e: e.memset(out, val), writes=w)

    def build(self):
        c, nc, P = self.c, self.nc, self.P
        D, KC = c.D, c.KC
        inp = lambda n, s, d=F32: nc.dram_tensor(n, list(s), d, kind="ExternalInput").ap()
        self.xp = inp("xp", [c.LK_P, D])
        self.xs = inp("xs", [c.LK_S, D])
        self.kbp = inp("kbp", [c.LK_P, 1])
        self.kbs = inp("kbs", [c.LK_S, 1])
        self.hval = inp("hval", [128, 2])
        self.memp = inp("memp", [c.NMEM, D])
        self.mems = inp("mems", [c.NMEM, D])
        self.w_in = inp("w_in", [D, c.INW])
        self.w_out = inp("w_out", [D, D])
        self.w_xq = inp("w_xq", [D, c.XW])
        self.w_xkv = inp("w_xkv", [D, 2 * c.XW])
        self.w_xo = inp("w_xo", [c.XW, D])
        self.w_up = inp("w_up", [D, 2 * c.DFF])
        self.w_down = inp("w_down", [c.DFF, D])
        self.gvec = {n: inp(n, [128, KC]) for n in ("norm_mix_g", "norm_x_g", "mem_norm_g", "norm_ffn_g")}
        self.final_g = inp("final_g", [D])
        self.gvecD = {n: inp(n + "_v", [D]) for n in ("norm_mix_g", "norm_x_g", "mem_norm_g", "norm_ffn_g")}
        self.sg_ln_g = inp("sg_ln_g", [c.AW])
        self.sg_ln_b = inp("sg_ln_b", [c.AW])
        self.grp_a_g = inp("grp_a_g", [c.AW])
        self.grp_b_g = inp("grp_b_g", [128, c.BH])
        self.sg_wT = inp("sg_wT", [128, c.AH, 128])
        self.sg_bT = inp("sg_bT", [128, c.AH])
        self.conv_w = inp("conv_w", [128, 3, 2 * c.GF])
        self.conv_b = inp("conv_b", [128, 2 * c.GF])
        self.ndm = inp("ndm", [128, 256])
        self.identm = inp("identm", [128, 128])
        self.yp = nc.dram_tensor("yp", [c.OWN, D], F32, kind="ExternalOutput").ap()
        self.ys = nc.dram_tensor("ys", [c.SEQ_S, D], F32, kind="ExternalOutput").ap()
        dt = lambda n, s, d: nc.dram_tensor(n, list(s), d).ap()
        self.seg = {}
        for sname, LK, LM, H, xin, kb, mem, yout, mh in (
            ("p", c.LK_P, c.LM_P, c.KH, self.xp, self.kbp, self.memp, self.yp, c.MH),
            ("s", c.LK_S, c.LM_S, 0, self.xs, self.kbs, self.mems, self.ys, 0),
        ):
            self.seg[sname] = dict(
                name=sname, LK=LK, LM=LM, H=H, x=xin, kb=kb, mem=mem, y=yout, MH=mh,
                Kd=dt("Kd" + sname, [c.BW, LK], BF16), Vd=dt("Vd" + sname, [c.BW, LK], BF16),
                Hd=dt("Hd" + sname, [D, LM], BF16), X2d=dt("X2d" + sname, [LM, D], F32))
        A2 = 2 * c.AW
        self.W = dict(
            k=WT(self, "k", self.w_in, A2 + c.BW, A2 + 2 * c.BW, "s"),
            vB=WT(self, "vB", self.w_in, A2 + 2 * c.BW, A2 + 3 * c.BW, "s"),
            xk=WT(self, "xk", self.w_xkv, 0, c.XW, "s"),
            xv=WT(self, "xv", self.w_xkv, c.XW, 2 * c.XW, "m"),
            u=WT(self, "u", self.w_in, 0, c.AW, "m"),
            vA=WT(self, "vA", self.w_in, c.AW, A2, "m"),
            q=WT(self, "q", self.w_in, A2, A2 + c.BW, "s"),
            o=WT(self, "o", self.w_out, 0, D, "m"),
            xq=WT(self, "xq", self.w_xq, 0, c.XW, "s"),
            xo=WT(self, "xo", self.w_xo, 0, D, "m"),
            up=WT(self, "up", self.w_up, 0, 2 * c.DFF, "s"),
            dn=WT(self, "dn", self.w_down, 0, D, "m"),
        )
        with contextlib.ExitStack() as st:
            self.st = st
            self.alloc()
            self.setup_consts()
            self.precast_q = []
            for n in ("k", "vB", "xk", "xv", "u", "vA", "q", "o", "xq", "xo", "up", "dn"):
                self.precast_q += self.W[n].precast_ops()
            self.drain_precast(len(self.W["k"].precast_ops()) + len(self.W["vB"].precast_ops()))
            for sname in ("p", "s"):
                self.kv_pass(self.seg[sname])
            n_late = len(self.W["up"].precast_ops()) + len(self.W["dn"].precast_ops())
            self.drain_precast(len(self.precast_q) - n_late)
            for sname in ("p", "s"):
                self.mem_kv(self.seg[sname])
                self.mixer_pass(self.seg[sname])
            self.drain_precast(10 ** 9)
            for sname in ("p", "s"):
                self.ffn_pass(self.seg[sname])
            P.emit(st)
        return nc

    def drain_precast(self, n):
        while n > 0 and self.precast_q:
            wt, cb, kp = self.precast_q.pop(0)
            wt.precast(cb, kp)
            n -= 1

    def carve(self):
        if hasattr(self, "_cv"):
            self.max_off = max(self.max_off, self._cv.off)
        self._cv = Carve(self)
        return self._cv

    def alloc(self):
        c, nc, st = self.c, self.nc, self.st
        D, KC = c.D, c.KC
        sb = lambda n, s, d: st.enter_context(nc.sbuf_tensor(n, list(s), d))
        self.wslots = [sb("ws%d" % i, [128, 4096], BF16) for i in range(self.NSLOT)]
        self.hA = sb("hA", [128, KC, 516], BF16)
        self.hAtok = "hA"
        self.ident = sb("ident", [128, 128], BF16)
        self.ones = sb("ones", [128, 128], BF16)
        self.zeros = sb("zeros", [128, 512], BF16)
        self.nd = sb("nd", [128, 256], F32)
        self.epsc = sb("epsc", [128, 1], F32)
        self.gT = {n: sb("gT_" + n, [128, KC], F32) for n in self.gvec}
        self.grpBT = sb("grpBT", [128, c.BH], F32)
        self.sgwT = sb("sgwT", [128, c.AH, 128], BF16)
        self.sgbT = sb("sgbT", [128, c.AH], F32)
        self.rsW = sb("rsW", [128, c.AH], F32)
        self.cw = sb("convw", [128, 3, 2 * c.GF], F32)
        self.cb = sb("convb", [128, 2 * c.GF], F32)
        self.hv = sb("hv", [128, 2], F32)
        self.stat = sb("stat", [128, 64 + 32 * 4], F32)
        self.Kx = sb("Kx", [128, c.XH, c.NMEM], BF16)
        self.Vx = sb("Vx", [128, 2, c.XW], BF16)
        self.arena = sb("arena", [128, self.arena_words], F32)
        ps = lambda n, s, d: st.enter_context(nc.psum_tensor(n, list(s), d))
        self.pb = [ps("pb%d" % i, [128, 512], F32) for i in range(6)]
        self.pt = [ps("pt%d" % i, [128, 1024], BF16) for i in range(2)]

    def vt_elems(self, T):
        tot = 0
        for d in DIL:
            nk = T // d + 128
            tot += (-(-nk // 128)) * d * 128
        return tot

    def setup_consts(self):
        c = self.c
        self.dma("pool", self.ident[:], self.identm, [], ["ident"], "k_ident")
        self.dma("pool", self.nd[:], self.ndm, [], ["nd"], "k_nd")
        self.dma("pool", self.sgwT[:], self.sg_wT, [], ["sgwT"], "k_sgwT")
        self.dma("pool", self.sgbT[:], self.sg_bT, [], ["sgbT"], "k_sgbT")
        self.dma("pool", self.hv[:], self.hval, [], ["hv"], "k_hv")
        for i, (n, t) in enumerate(self.gT.items()):
            self.dma("pool", t[:], self.gvec[n], [], ["gT_" + n], "k_gT_" + n)
        self.dma("pool", self.grpBT[:], self.grp_b_g, [], ["grpBT"], "k_grpBT")
        self.dma("pool", self.cw[:], self.conv_w, [], ["convw"], "k_convw")
        self.dma("pool", self.cb[:], self.conv_b, [], ["convb"], "k_convb")
        self.mset(self.ones[:], 1.0, ["ones"])
        self.mset(self.zeros[:], 0.0, ["zeros"])
        self.mset(self.epsc[:], EPS, ["epsc"])
        for g in range(c.AH):
            self.mm(self.pb[4][:, g:g + 1], self.sgwT[:, g, :], self.ones[:, 0:1], True, True, ["sgwT", "ones"], [("pb", 4)])
        self.vcopy(self.rsW[:], self.pb[4][:, 0:c.AH], [("pb", 4)], ["rsW"])

    def load_grep(self, buf, gname):
        self.dma("pool", buf, self.gvecD[gname].partition_broadcast(128), [], ["grep"], ("cA", "grep"))

    def make_h(self, src_tile, src_tok, xs_buf, xs_tok, grep, col0, stat_col=0, h=None, htok=None):
        c = self.c
        D, KC = c.D, c.KC
        if h is None:
            h, htok = self.hA, self.hAtok
        ssq = self.stat[:, stat_col:stat_col + 1]
        rt = self.stat[:, stat_col + 1:stat_col + 2]
        rstd = self.stat[:, stat_col + 2:stat_col + 3]
        s0, s1, s2 = ("st", stat_col), ("st", stat_col + 1), ("st", stat_col + 2)
        self.act(xs_buf, src_tile, AF.Square, [src_tok], [xs_tok, s0], accum_out=ssq)
        self.act(rt, ssq, AF.Sqrt, [s0, "epsc"], [s1], bias=self.epsc[:, 0:1], scale=1.0 / D)
        self.rcp(rstd, rt, [s1], [s2])
        self.stt(xs_buf, src_tile, rstd, grep, ALU.mult, ALU.mult, [src_tok, s2, "grep"], [xs_tok])
        G = min(8, KC)
        for kg in range(KC // G):
            b = self.rr("pt", 2)
            for j in range(G):
                kc = kg * G + j
                self.tr(self.pt[b][:, j * 128:(j + 1) * 128], xs_buf[:, kc * 128:(kc + 1) * 128], [xs_tok], [("pt", b)])
            src = self.pt[b][:, 0:G * 128].rearrange("p (a b) -> p a b", b=128)
            dst = h[:, kg * G:(kg + 1) * G, col0:col0 + 128]
            if self.rr("mh_ev", 2):
                self.vcopy(dst, src, [("pt", b)], [htok])
            else:
                self.acopy(dst, src, [("pt", b)], [htok])

    def linear_fm(self, wt, h, htok, T, evac, tcol0=0):
        ngrp = wt.cw // 128
        for cb in range(wt.ncb):
            banks = [self.rr("pbl", 4) for _ in range(ngrp)]
            for kp in range(wt.nkp):
                view, tok = wt.load(cb, kp)
                for g in range(ngrp):
                    for kc in range(wt.kcs(kp)):
                        kk = kp * wt.kcp + kc
                        self.mm(self.pb[banks[g]][:, 0:T], view[:, kc, g * 128:(g + 1) * 128], h[:, kk, tcol0:tcol0 + T],
                                kk == 0, kk == wt.KC - 1, [tok, htok], [("pb", banks[g])])
            for g in range(ngrp):
                evac(cb * ngrp + g, self.pb[banks[g]][:, 0:T], ("pb", banks[g]))

    def linear_tm(self, wt, h, htok, ntt, evac):
        for cb in range(wt.ncb):
            for kp in range(wt.nkp):
                view, tok = wt.load(cb, kp)
                for tt in range(ntt):
                    for kc in range(wt.kcs(kp)):
                        kk = kp * wt.kcp + kc
                        self.mm(self.pb[tt][:, 0:wt.cw], h[:, kk, tt * 128:(tt + 1) * 128], view[:, kc, :],
                                kk == 0, kk == wt.KC - 1, [tok, htok], [("pb", tt)])
                    if kp == wt.nkp - 1:
                        evac(cb, tt, self.pb[tt][:, 0:wt.cw], ("pb", tt))

    def kv_pass(self, sg):
        c, P = self.c, self.P
        D = c.D
        P.barrier()
        cv = self.carve()
        xt = [cv.f32([128, D]) for _ in range(4)]
        xsb = [cv.bf16([128, D]) for _ in range(2)]
        kst = [cv.bf16([128, 512]) for _ in range(4)]
        vst = [cv.bf16([128, 512]) for _ in range(4)]
        grep = cv.f32([128, D])
        self.load_grep(grep, "norm_mix_g")
        ntiles = sg["LK"] // 128
        hA, htok = self.hA, self.hAtok
        wv = self.W["vB"]
        sn = sg["name"]
        blocks = []
        t0 = 0
        while t0 < ntiles:
            ntt = min(4, ntiles - t0)
            blocks.append((t0, ntt))
            t0 += ntt

        def issue_loads(blk):
            t0, ntt = blk
            for tt in range(ntt):
                k = t0 * 128 + tt * 128
                self.dma("pool", xt[tt], sg["x"][k:k + 128, :], [], [("xt", tt)], ("xt", tt))
        issue_loads(blocks[0])
        for bi, (t0, ntt) in enumerate(blocks):
            T = ntt * 128
            k0 = t0 * 128
            for tt in range(ntt):
                i = tt % 2
                self.make_h(xt[tt], ("xt", tt), xsb[i], ("xs", i), grep, tt * 128, stat_col=4 * i)
            if bi + 1 < len(blocks):
                issue_loads(blocks[bi + 1])
            self.drain_precast(8)

            def evk(g, pap, ptok, T=T, k0=k0):
                i = self.rr("kst", 4)
                self.acopy(kst[i][:, 0:T], pap, [ptok], [("kst", i)])
                self.dma("pool", sg["Kd"][g * 128:(g + 1) * 128, k0:k0 + T], kst[i][:, 0:T], [("kst", i)], [("Kd", sn)], ("kst", i))
            self.linear_fm(self.W["k"], hA, htok, T, evk)

            def evv(g, pap, ptok, T=T, k0=k0):
                i = self.rr("vst", 4)
                self.vcopy(vst[i][:, 0:T], pap, [ptok], [("vst", i)])
                self.dma("pool", sg["Vd"][g * 128:(g + 1) * 128, k0:k0 + T], vst[i][:, 0:T], [("vst", i)], [("Vd", sn)], ("vst", i))
            self.linear_fm(wv, hA, htok, T, evv)

    def mem_kv(self, sg):
        c, P = self.c, self.P
        D = c.D
        P.barrier()
        cv = self.carve()
        xt = [cv.f32([128, D]) for _ in range(2)]
        xsb = [cv.bf16([128, D]) for _ in range(2)]
        hA, htok = self.hA, self.hAtok
        grep = cv.f32([128, D])
        self.load_grep(grep, "mem_norm_g")
        for tt in range(2):
            self.dma("pool", xt[tt], sg["mem"][tt * 128:(tt + 1) * 128, :], [], [("xt", tt)], ("xt", tt))
            self.make_h(xt[tt], ("xt", tt), xsb[tt], ("xs", tt), grep, tt * 128, stat_col=4 * tt)

        def evk(g, pap, ptok):
            self.acopy(self.Kx[:, g, :], pap, [ptok], ["Kx"])
        self.linear_fm(self.W["xk"], hA, htok, c.NMEM, evk)
        wv = self.W["xv"]

        def evv(cb, tt, pap, ptok):
            self.acopy(self.Vx[:, tt, cb * wv.cw:(cb + 1) * wv.cw], pap, [ptok], ["Vx"])
        self.linear_tm(wv, hA, htok, 2, evv)

    def mixer_pass(self, sg):
        c = self.c
        ntiles = sg["LM"] // 128
        t0 = 0
        while t0 < ntiles:
            ntt = min(c.TM, ntiles - t0)
            self.mixer_block(sg, t0 * 128, ntt)
            t0 += ntt

    def mixer_block(self, sg, tau0, ntt):
        c, P = self.c, self.P
        D, KC = c.D, c.KC
        T = ntt * 128
        TMX = c.TM * 128
        H = sg["H"]
        sn = sg["name"]
        hA, hAtok = self.hA, self.hAtok
        scale = 128 ** -0.5
        st = self.stat
        P.barrier()
        cv = self.carve()
        hB = cv.bf16([128, KC, TMX])
        hBtok = "hB"
        base = cv.off
        xt = [cv.f32([128, D]) for _ in range(2)]
        xsb = [cv.bf16([128, D]) for _ in range(2)]
        xrow = lambda tt: sg["x"][H + tau0 + tt * 128: H + tau0 + (tt + 1) * 128, :]
        grepN = cv.f32([128, D])
        self.load_grep(grepN, "norm_mix_g")
        for tt in range(ntt):
            i = tt % 2
            self.dma("pool", xt[i], xrow(tt), [], [("xt", i)], ("xt", i))
            self.make_h(xt[i], ("xt", i), xsb[i], ("xs", i), grepN, tt * 128, stat_col=4 * i)
        P.barrier()
        self.max_off = max(self.max_off, cv.off)
        cv.off = base
        AW, AH = c.AW, c.AH
        u = cv.bf16([128, c.TM, AW])
        vA = cv.bf16([128, c.TM, AW])
        lnG = cv.f32([128, AW])
        lnB = cv.f32([128, AW])
        grA = cv.f32([128, AW])
        Cb = cv.f32([128, AW])
        tmpA = [cv.f32([128, 512]) for _ in range(2)]
        aouts = [lnB] + [cv.f32([128, AW]) for _ in range(c.TM - 1)]
        ans = [cv.bf16([128, AW]) for _ in range(c.TM)]
        for dst, src, nm in ((lnG, self.sg_ln_g, "lnG"), (lnB, self.sg_ln_b, "lnB"), (grA, self.grp_a_g, "grA")):
            self.dma("pool", dst, src.partition_broadcast(128), [], [nm], ("cA", nm))
        for g in range(AH):
            self.ts(Cb[:, g * 128:(g + 1) * 128], lnB[:, g * 128:(g + 1) * 128], self.rsW[:, g:g + 1], self.sgbT[:, g:g + 1],
                    ALU.mult, ALU.add, ["lnB", "rsW", "sgbT"], ["Cb"])
        self.drain_precast(11)
        wu, wva = self.W["u"], self.W["vA"]

        def evu(cb, tt, pap, ptok):
            self.act(u[:, tt, cb * wu.cw:(cb + 1) * wu.cw], pap, AF.Gelu, [ptok], [("u", tt)])
        self.linear_tm(wu, hA, hAtok, ntt, evu)

        def evva(cb, tt, pap, ptok):
            self.act(vA[:, tt, cb * wva.cw:(cb + 1) * wva.cw], pap, AF.Gelu, [ptok], [("vA", tt)])
        self.linear_tm(wva, hA, hAtok, ntt, evva)
        bw = min(512, max(64, AW // 2))
        nbn = AW // bw
        S0 = 64
        sc = lambda tt, j: S0 + 32 * tt + j
        for tt in range(ntt):
            for j in range(nbn):
                P.op("dve", (lambda o, i: (lambda e: e.bn_stats(out=o, in_=i)))(st[:, sc(tt, 8 + 6 * j):sc(tt, 14 + 6 * j)], vA[:, tt, j * bw:(j + 1) * bw]),
                     reads=[("vA", tt)], writes=[("bn", tt, j)])
            P.op("dve", (lambda o, i: (lambda e: e.bn_aggr(out=o, in_=i)))(st[:, sc(tt, 0):sc(tt, 2)],
                                                                         st[:, sc(tt, 8):sc(tt, 8 + 6 * nbn)].rearrange("p (a b) -> p a b", b=6)),
                 reads=[("bn", tt, j) for j in range(nbn)], writes=[("mv", tt)])
        for tt in range(ntt):
            self.act(st[:, sc(tt, 2):sc(tt, 3)], st[:, sc(tt, 1):sc(tt, 2)], AF.Sqrt, [("mv", tt), "epsc"], [("lnr", tt)], bias=self.epsc[:, 0:1], scale=1.0)
        for tt in range(ntt):
            self.rcp(st[:, sc(tt, 3):sc(tt, 4)], st[:, sc(tt, 2):sc(tt, 3)], [("lnr", tt)], [("lnr2", tt)])
        for tt in range(ntt):
            self.ts(vA[:, tt, :], vA[:, tt, :], st[:, sc(tt, 0):sc(tt, 1)], st[:, sc(tt, 3):sc(tt, 4)], ALU.subtract, ALU.mult,
                    [("vA", tt), ("mv", tt), ("lnr2", tt)], [("vA", tt)])
        hpb = min(4, AH)
        for tt in range(ntt):
            aout = aouts[tt]
            for g0 in range(0, AH, hpb):
                bk = 4 + self.rr("pbm", 2)
                for g in range(g0, g0 + hpb):
                    self.mm(self.pb[bk][:, (g - g0) * 128:(g - g0 + 1) * 128], self.sgwT[:, g, :], vA[:, tt, g * 128:(g + 1) * 128],
                            True, True, ["sgwT", ("vA", tt)], [("pb", bk)])
                cols = slice(g0 * 128, (g0 + hpb) * 128)
                n = hpb * 128
                ti = self.rr("tmpA", 2)
                tA, ttok = tmpA[ti], ("tmpA", ti)
                self.tt(tA[:, 0:n], self.pb[bk][:, 0:n], lnG[:, cols], ALU.mult, [("pb", bk), "lnG"], [ttok])
                self.tt(tA[:, 0:n], tA[:, 0:n], Cb[:, cols], ALU.add, [ttok, "Cb"], [ttok], eng="pool")
                wtoks = [("aout", tt)] + (["lnB"] if tt == 0 else [])
                self.tt(aout[:, cols], tA[:, 0:n], u[:, tt, cols], ALU.mult, [ttok, ("u", tt)], wtoks)
        for tt in range(ntt):
            self.act(ans[tt][:], aouts[tt][:], AF.Square, [("aout", tt)], [("an", tt), ("ssA", tt)], accum_out=st[:, sc(tt, 4):sc(tt, 5)])
        for tt in range(ntt):
            self.act(st[:, sc(tt, 5):sc(tt, 6)], st[:, sc(tt, 4):sc(tt, 5)], AF.Sqrt, [("ssA", tt), "epsc"], [("ssA2", tt)],
                     bias=self.epsc[:, 0:1], scale=1.0 / AW)
        for tt in range(ntt):
            self.rcp(st[:, sc(tt, 6):sc(tt, 7)], st[:, sc(tt, 5):sc(tt, 6)], [("ssA2", tt)], [("ssA3", tt)])
        for tt in range(ntt):
            self.stt(ans[tt][:], aouts[tt][:], st[:, sc(tt, 6):sc(tt, 7)], grA[:], ALU.mult, ALU.mult,
                     [("aout", tt), ("ssA3", tt), "grA"], [("an", tt)])
        G = min(8, AH)
        for tt in range(ntt):
            for kg in range(AH // G):
                b = self.rr("pt", 2)
                for j in range(G):
                    g = kg * G + j
                    self.tr(self.pt[b][:, j * 128:(j + 1) * 128], ans[tt][:, g * 128:(g + 1) * 128], [("an", tt)], [("pt", b)])
                self.acopy(hB[:, kg * G:(kg + 1) * G, tt * 128:(tt + 1) * 128],
                           self.pt[b][:, 0:G * 128].rearrange("p (a b) -> p a b", b=128), [("pt", b)], [hBtok])
        P.barrier()
        self.max_off = max(self.max_off, cv.off)
        cv.off = base
        BH = c.BH
        q = cv.bf16([128, BH, TMX])
        WK = TMX + 2 * c.KH
        kwin = [cv.bf16([128, WK]) for _ in range(2)]
        vwin = [cv.bf16([128, WK]) for _ in range(2)]
        ch = self.attn_chunks(sg, tau0, T)
        nve = self.vt_elems(TMX)
        assert ch["vtot"] <= nve
        vts = [cv.bf16([128, nve]) for _ in range(2)]
        bo = cv.bf16([128, BH, TMX])
        kbt = cv.f32([128, 64])
        ndk_off = {}
        _o = 0
        for it in ch["bitems"]:
            ndk_off[(it[0], it[1])] = _o
            _o += it[0] * it[4]
        ndk = cv.f32([128, max(_o, 1)])
        NS = max(TMX, 256)
        ptS = [cv.bf16([128, NS]) for _ in range(3)]
        tmpS = [cv.f32([128, NS]) for _ in range(3)]
        rden = cv.f32([128, TMX])
        sq = cv.bf16([128, TMX])
        rstdB = cv.f32([128, TMX])

        def evq(g, pap, ptok):
            self.acopy(q[:, g, 0:T], pap, [ptok], [("q", g)])
        self.linear_fm(self.W["q"], hA, hAtok, T, evq)
        assert ch["ncol"] <= 48
        for (d, a, nk, ci) in ch["vload"]:
            src = bass.AP(sg["kb"].tensor, sg["kb"].offset + d * a, [[d, nk], [1, d]])
            self.dma("pool", kbt[0:nk, ci:ci + d], src, [], ["kbt"], ("kbt", 0))
        for (d, a, nk, qa, nq, ci) in ch["bitems"]:
            J0 = qa - a + 64
            o0 = ndk_off[(d, a)]
            if d > 1:
                self.tt(ndk[0:nk, o0:o0 + d * nq].rearrange("p (r q) -> p r q", q=nq),
                        self.nd[0:nk, J0:J0 + nq].unsqueeze(1).to_broadcast([nk, d, nq]),
                        kbt[0:nk, ci:ci + d].unsqueeze(2).to_broadcast([nk, d, nq]), ALU.add, ["nd", "kbt"], ["ndk"])
            else:
                self.ts(ndk[0:nk, o0:o0 + nq], self.nd[0:nk, J0:J0 + nq], kbt[0:nk, ci:ci + 1], None, ALU.add, None, ["nd", "kbt"], ["ndk"])
        wlo, wn = ch["wlo"], ch["wn"]
        def load_head(h):
            bi = h % 2
            self.dma("pool", kwin[bi][:, 0:wn], sg["Kd"][h * 128:(h + 1) * 128, wlo:wlo + wn], [("Kd", sn)], [("kwin", bi)], ("kwin", bi))
            self.dma("pool", vwin[bi][:, 0:wn], sg["Vd"][h * 128:(h + 1) * 128, wlo:wlo + wn], [("Vd", sn)], [("vwin", bi)], ("vwin", bi))

        def tr_head(h):
            bi = h % 2
            tiles = []
            for (d, a, nk, ci) in ch["vload"]:
                for rho in range(d):
                    tiles.append((d, a, nk, rho, ch["voff"][(d, a)] + rho * 128))
            ti = 0
            while ti < len(tiles):
                nk0 = tiles[ti][2]
                grp = [tiles[ti]]
                while len(grp) < 8 and ti + len(grp) < len(tiles) and tiles[ti + len(grp)][2] == nk0:
                    grp.append(tiles[ti + len(grp)])
                b = self.rr("pt", 2)
                for j, (d, a, nk, rho, off) in enumerate(grp):
                    c0 = rho + d * a - wlo
                    self.tr(self.pt[b][0:nk, j * 128:(j + 1) * 128], vwin[bi][:, c0:c0 + d * (nk - 1) + 1:d], [("vwin", bi)], [("pt", b)])
                off0 = grp[0][4]
                self.acopy(vts[bi][0:nk0, off0:off0 + len(grp) * 128], self.pt[b][0:nk0, 0:len(grp) * 128], [("pt", b)], [("vt", bi)])
                ti += len(grp)
        load_head(0)
        tr_head(0)
        for h in range(BH):
            bi = h % 2
            if h + 1 < BH:
                load_head(h + 1)
            slope = 2.0 ** (-8.0 * (h + 1) / BH)
            pend = []
            zeroed = [False]

            def zero_init():
                if not zeroed[0]:
                    zeroed[0] = True
                    for bk in (4, 5):
                        self.mm(self.pb[bk][:, 0:T], self.zeros[:, 0:128], self.zeros[:, 0:T], True, False, ["zeros"], [("pb", bk)])

            def pv(it, si, bi=bi):
                (d, a, nk, qa, nq, ci) = it
                zero_init()
                for rho in range(d):
                    off = ch["voff"][(d, a)] + rho * 128
                    qcol0 = rho + d * qa - H - tau0
                    cols = slice(qcol0, qcol0 + d * (nq - 1) + 1, d)
                    pslice = ptS[si][0:nk, rho * nq:(rho + 1) * nq]
                    self.mm(self.pb[4][:, cols], vts[bi][0:nk, off:off + 128], pslice, False, False,
                            [("vt", bi), ("ptS", si)], [("pb", 4)], skip_group_check=True)
                    if not (DEN_MERGE and d > 1):
                        self.mm(self.pb[5][:, cols], self.ones[0:nk, :], pslice, False, False,
                                ["ones", ("ptS", si)], [("pb", 5)], skip_group_check=True)
                if DEN_MERGE and d > 1:
                    base = d * qa - H - tau0
                    self.mm(self.pb[5][:, base:base + d * nq].rearrange("p (j r) -> p r j", r=d), self.ones[0:nk, :],
                            ptS[si][0:nk, 0:d * nq].rearrange("p (r q) -> p r q", q=nq), False, False,
                            ["ones", ("ptS", si)], [("pb", 5)], skip_group_check=True)
            for n_it, it in enumerate(ch["bitems"]):
                (d, a, nk, qa, nq, ci) = it
                si = n_it % 3
                n = d * nq
                assert n <= NS
                for rho in range(d):
                    kcol0 = rho + d * a - wlo
                    kcols = slice(kcol0, kcol0 + d * (nk - 1) + 1, d)
                    qcol0 = rho + d * qa - H - tau0
                    qcols = slice(qcol0, qcol0 + d * (nq - 1) + 1, d)
                    self.mm(self.pb[si][0:nk, rho * nq:(rho + 1) * nq], kwin[bi][:, kcols], q[:, h, qcols], True, True,
                            [("kwin", bi), ("q", h)], [("pb", si)])
                o0 = ndk_off[(d, a)]
                self.stt(tmpS[si][0:nk, 0:n], ndk[0:nk, o0:o0 + n], slope * d / scale, self.pb[si][0:nk, 0:n], ALU.mult, ALU.add,
                         ["ndk", ("pb", si)], [("tmpS", si)])
                self.act(ptS[si][0:nk, 0:n], tmpS[si][0:nk, 0:n], AF.Exp, [("tmpS", si)], [("ptS", si)], scale=scale)
                pend.append((it, si))
                if len(pend) > 2:
                    pv(*pend.pop(0))
                if n_it == 3 and h + 1 < BH:
                    tr_head(h + 1)
            if len(ch["bitems"]) <= 3 and h + 1 < BH:
                tr_head(h + 1)
            while pend:
                pv(*pend.pop(0))
            self.rcp(rden[:, 0:T], self.pb[5][:, 0:T], [("pb", 5)], ["rden"])
            self.tt(bo[:, h, 0:T], self.pb[4][:, 0:T], rden[:, 0:T], ALU.mult, [("pb", 4), "rden"], [("bo", h)])
            self.act(sq[:, 0:T], bo[:, h, 0:T], AF.Square, [("bo", h)], ["sq"])
            self.mm(self.pb[3][:, 0:T], self.ones[:, :], sq[:, 0:T], h == 0, h == BH - 1, ["ones", "sq"], [("pb", 3)])
        self.act(rstdB[:, 0:T], self.pb[3][:, 0:T], AF.Sqrt, [("pb", 3), "epsc"], ["rstdB"], bias=self.epsc[:, 0:1], scale=1.0 / c.BW)
        self.rcp(rstdB[:, 0:T], rstdB[:, 0:T], ["rstdB"], ["rstdB"])
        for h in range(BH):
            self.stt(hB[:, AH + h, 0:T], bo[:, h, 0:T], self.grpBT[:, h:h + 1], rstdB[:, 0:T], ALU.mult, ALU.mult,
                     [("bo", h), "grpBT", "rstdB"], [hBtok])
        P.barrier()
        self.max_off = max(self.max_off, cv.off)
        cv.off = base
        xres = cv.f32([128, c.TM, D])
        xsb = [cv.bf16([128, D]) for _ in range(2)]
        qx = cv.bf16([128, c.XH, TMX])
        ptX = [cv.bf16([128, TMX]) for _ in range(2)]
        rdx = cv.f32([128, TMX])
        grepO = cv.f32([128, D])
        self.load_grep(grepO, "norm_x_g")
        for tt in range(ntt):
            self.dma("pool", xres[:, tt, :], xrow(tt), [], [("xres", tt)], ("xres", tt))
        self.drain_precast(11)
        wo = self.W["o"]

        def evo(cb, tt, pap, ptok):
            dst = xres[:, tt, cb * wo.cw:(cb + 1) * wo.cw]
            self.tt(dst, pap, dst, ALU.add, [ptok, ("xres", tt)], [("xres", tt)])
        self.linear_tm(wo, hB, hBtok, ntt, evo)
        for tt in range(ntt):
            i = tt % 2
            self.make_h(xres[:, tt, :], ("xres", tt), xsb[i], ("xs", i), grepO, tt * 128, stat_col=4 * i)

        def evxq(g, pap, ptok):
            self.acopy(qx[:, g, 0:T], pap, [ptok], [("qx", g)])
        self.linear_fm(self.W["xq"], hA, hAtok, T, evxq)
        for xh in range(c.XH):
            for mc in range(2):
                self.mm(self.pb[mc][:, 0:T], self.Kx[:, xh, mc * 128:(mc + 1) * 128], qx[:, xh, 0:T], True, True,
                        ["Kx", ("qx", xh)], [("pb", mc)])
                self.act(ptX[mc][:, 0:T], self.pb[mc][:, 0:T], AF.Exp, [("pb", mc)], [("ptX", mc)], scale=scale)
            for mc in range(2):
                self.mm(self.pb[4][:, 0:T], self.Vx[:, mc, xh * 128:(xh + 1) * 128], ptX[mc][:, 0:T], mc == 0, mc == 1,
                        ["Vx", ("ptX", mc)], [("pb", 4)])
                self.mm(self.pb[5][:, 0:T], self.ones[:, :], ptX[mc][:, 0:T], mc == 0, mc == 1, ["ones", ("ptX", mc)], [("pb", 5)])
            self.rcp(rdx[:, 0:T], self.pb[5][:, 0:T], [("pb", 5)], ["rdx"])
            self.tt(hB[:, xh, 0:T], self.pb[4][:, 0:T], rdx[:, 0:T], ALU.mult, [("pb", 4), "rdx"], [hBtok])
        self.load_grep(grepO, "norm_ffn_g")
        wxo = self.W["xo"]

        def evxo(cb, tt, pap, ptok):
            dst = xres[:, tt, cb * wxo.cw:(cb + 1) * wxo.cw]
            self.tt(dst, pap, dst, ALU.add, [ptok, ("xres", tt)], [("xres", tt)])
        self.linear_tm(wxo, hB, hBtok, ntt, evxo)
        for tt in range(ntt):
            self.dma("pool", sg["X2d"][tau0 + tt * 128:tau0 + (tt + 1) * 128, :], xres[:, tt, :], [("xres", tt)], [("X2d", sn)], ("x2d", tt % 2))
            i = tt % 2
            self.make_h(xres[:, tt, :], ("xres", tt), xsb[i], ("xs", i), grepO, tt * 128, stat_col=4 * i)
        self.dma("pool", sg["Hd"].rearrange("(kc p) t -> p kc t", p=128)[:, :, tau0:tau0 + T], hA[:, :, 0:T], [hAtok], [("Hd", sn)], ("hd", 0))
        self.max_off = max(self.max_off, cv.off)

    def attn_chunks(self, sg, tau0, T):
        H, LK = sg["H"], sg["LK"]
        bitems, vload, voff = [], [], {}
        off = 0
        ci = 0
        wlo, whi = 10 ** 9, 0
        for d in DIL:
            m_lo = (tau0 + H) // d
            nqs = T // d
            k_lo = max(0, m_lo - 64)
            k_hi = min(LK // d, m_lo + nqs + 64)
            a = k_lo
            while a < k_hi:
                nk = min(128, k_hi - a)
                vload.append((d, a, nk, ci))
                voff[(d, a)] = off
                off += d * 128
                qa = max(m_lo, a - 64)
                qb = min(m_lo + nqs, a + nk + 64)
                if qb > qa:
                    bitems.append((d, a, nk, qa, qb - qa, ci))
                ci += d
                wlo = min(wlo, d * a)
                whi = max(whi, d * (a + nk))
                a += nk
        assert ci <= 64
        bitems.sort(key=lambda it: -it[0])
        return dict(bitems=bitems, vload=vload, voff=voff, wlo=wlo, wn=whi - wlo, vtot=off, ncol=ci)

    def ffn_pass(self, sg):
        nblk = (sg["LM"] - 2 * sg["MH"]) // 512
        for b in range(nblk):
            self.ffn_block(sg, b, nblk)

    def ffn_block(self, sg, b, nblk):
        c, P = self.c, self.P
        D, KC, GF = c.D, c.KC, c.GF
        MH = sg["MH"]
        sn = sg["name"]
        tau = MH + b * 512
        hA, hAtok = self.hA, self.hAtok
        P.barrier()
        cv = self.carve()
        xres = cv.f32([128, 4, D])
        gch = [cv.bf16([128, c.FFG, 512]) for _ in range(2)]
        tg = [cv.f32([128, 256]) for _ in range(4)]
        tv = [cv.f32([128, 256]) for _ in range(4)]
        fg = cv.f32([128, D])
        ysb = [cv.bf16([128, D])] * 2
        Hv = sg["Hd"].rearrange("(kc p) t -> p kc t", p=128)
        lo, hi = tau - 1, tau + 513
        clo, chi = max(lo, 0), min(hi, sg["LM"])
        self.dma("pool", hA[:, :, clo - lo:chi - lo], Hv[:, :, clo:chi], [("Hd", sn)], [hAtok], ("hd", 1))
        if lo < 0:
            self.mset(hA[:, :, 0:1], 0.0, [hAtok])
        elif b == 0:
            self.ts(hA[:, :, 0:1], hA[:, :, 0:1], self.hv[:, 0:1], None, ALU.mult, None, [hAtok, "hv"], [hAtok])
        if hi > sg["LM"]:
            self.mset(hA[:, :, 513:514], 0.0, [hAtok])
        elif b == nblk - 1:
            self.ts(hA[:, :, 513:514], hA[:, :, 513:514], self.hv[:, 1:2], None, ALU.mult, None, [hAtok, "hv"], [hAtok])
        for tt in range(4):
            self.dma("pool", xres[:, tt, :], sg["X2d"][tau + tt * 128:tau + (tt + 1) * 128, :], [("X2d", sn)], [("xres", tt)], ("xres", tt))
        self.dma("pool", fg, self.final_g.partition_broadcast(128), [], ["fg"], ("cA", "fg"))
        wup, wdn = self.W["up"], self.W["dn"]
        gpp = wup.cw // 128
        halves = ((0, 258), (256, 258))
        f0 = 0
        while f0 < GF:
            nf = min(c.FFG, GF - f0)
            gi = self.rr("gch", 2)
            gbuf, gtok = gch[gi], ("gch", gi)
            grp = f0
            while grp < f0 + nf:
                ng = min(gpp, f0 + nf - grp)
                res = {}
                for which, gbase in (("g", 0), ("v", GF)):
                    cg0 = gbase + grp
                    cbi, gin = divmod(cg0, gpp)
                    assert gin + ng <= gpp
                    for kp in range(wup.nkp):
                        view, tok = wup.load(cbi, kp)
                        for j in range(ng):
                            for hf, (c0, n) in enumerate(halves):
                                bk = j * 2 + hf
                                for kc in range(wup.kcs(kp)):
                                    kk = kp * wup.kcp + kc
                                    self.mm(self.pb[bk][:, 0:n], view[:, kc, (gin + j) * 128:(gin + j + 1) * 128], hA[:, kk, c0:c0 + n],
                                            kk == 0, kk == KC - 1, [tok, hAtok], [("pb", bk)])
                    for j in range(ng):
                        for hf in range(2):
                            bk = j * 2 + hf
                            cg = cg0 + j
                            tb = (tg if which == "g" else tv)[bk]
                            ttok = ("t" + which, bk)
                            pbk = self.pb[bk]
                            self.act(tb[:], pbk[:, 1:257], AF.Identity, [("pb", bk), "convw", "convb"], [ttok],
                                     bias=self.cb[:, cg:cg + 1], scale=self.cw[:, 1, cg:cg + 1])
                            self.stt(tb[:], pbk[:, 0:256], self.cw[:, 0, cg:cg + 1], tb[:], ALU.mult, ALU.add, [("pb", bk), "convw", ttok], [ttok])
                            self.stt(tb[:], pbk[:, 2:258], self.cw[:, 2, cg:cg + 1], tb[:], ALU.mult, ALU.add, [("pb", bk), "convw", ttok], [ttok])
                            res[(which, j, hf)] = (tb, ttok)
                for j in range(ng):
                    for hf in range(2):
                        tbg, tokg = res[("g", j, hf)]
                        tbv, tokv = res[("v", j, hf)]
                        self.act(tbg[:], tbg[:], AF.Silu, [tokg], [tokg])
                        self.tt(gbuf[:, grp - f0 + j, hf * 256:(hf + 1) * 256], tbg[:], tbv[:], ALU.mult, [tokg, tokv], [gtok])
                grp += ng
            kp0, kp1 = f0 // wdn.kcp, -(-(f0 + nf) // wdn.kcp)
            for cbd in range(wdn.ncb):
                views = [wdn.load(cbd, kp) for kp in range(kp0, kp1)]
                for tt in range(4):
                    bk = 4 + self.rr("pbd", 2)
                    for jf in range(nf):
                        view, tok = views[(f0 + jf) // wdn.kcp - kp0]
                        kc = (f0 + jf) % wdn.kcp
                        self.mm(self.pb[bk][:, 0:wdn.cw], gbuf[:, jf, tt * 128:(tt + 1) * 128], view[:, kc, :], jf == 0, jf == nf - 1,
                                [tok, gtok], [("pb", bk)])
                    dst = xres[:, tt, cbd * wdn.cw:(cbd + 1) * wdn.cw]
                    self.tt(dst, self.pb[bk][:, 0:wdn.cw], dst, ALU.add, [("pb", bk), ("xres", tt)], [("xres", tt)])
            f0 += nf
        st = self.stat
        for tt in range(4):
            i = tt % 2
            xr = xres[:, tt, :]
            s = 40 + 4 * i
            self.act(ysb[i][:], xr, AF.Square, [("xres", tt)], ["ysb", ("fst", i)], accum_out=st[:, s:s + 1])
            self.act(st[:, s + 1:s + 2], st[:, s:s + 1], AF.Sqrt, [("fst", i), "epsc"], [("fst1", i)], bias=self.epsc[:, 0:1], scale=1.0 / D)
            self.rcp(st[:, s + 2:s + 3], st[:, s + 1:s + 2], [("fst1", i)], [("fst2", i)])
            self.stt(xr, xr, st[:, s + 2:s + 3], fg[:], ALU.mult, ALU.mult, [("xres", tt), ("fst2", i), "fg"], [("xres", tt)])
            self.dma("pool", sg["y"][b * 512 + tt * 128:b * 512 + (tt + 1) * 128, :], xr, [("xres", tt)], [], ("yout", tt % 2))
        self.max_off = max(self.max_off, cv.off)


def _nd_matrix():
    i = np.arange(128)[:, None]
    J = np.arange(256)[None, :]
    dist = np.abs(J - 64 - i)
    return np.where(dist <= 64, -dist, -NEGBIG).astype(np.float32)


def run(cfg, inputs):
    c = cfg
    x_prompt = np.asarray(inputs["x_prompt"], np.float32)
    x_sample = np.asarray(inputs["x_sample"], np.float32)
    D = c.D
    nc = Kern(cfg).build()
    f = lambda n: np.ascontiguousarray(np.asarray(inputs[n], np.float32)[0])
    pk = lambda v: np.ascontiguousarray(v.reshape(-1, 128).T)
    shared = dict(
        w_in=f("w_in"), w_out=f("w_out"), w_xq=f("w_xq"), w_xkv=f("w_xkv"), w_xo=f("w_xo"), w_up=f("w_up"), w_down=f("w_down"),
        norm_mix_g=pk(f("norm_mix_g")), norm_x_g=pk(f("norm_x_g")), mem_norm_g=pk(f("mem_norm_g")), norm_ffn_g=pk(f("norm_ffn_g")),
        final_g=np.ascontiguousarray(np.asarray(inputs["final_g"], np.float32)),
        norm_mix_g_v=f("norm_mix_g"), norm_x_g_v=f("norm_x_g"), mem_norm_g_v=f("mem_norm_g"), norm_ffn_g_v=f("norm_ffn_g"),
        sg_ln_g=f("sg_ln_g"), sg_ln_b=f("sg_ln_b"), grp_a_g=f("grp_a_g"), grp_b_g=pk(f("grp_b_g")),
        sg_wT=np.ascontiguousarray(np.transpose(f("sg_w"), (2, 0, 1))),
        sg_bT=np.ascontiguousarray(f("sg_b").T),
        conv_w=np.ascontiguousarray(np.transpose(f("conv_w").reshape(3, -1, 128), (2, 0, 1))), conv_b=pk(f("conv_b")),
        ndm=_nd_matrix(), identm=np.eye(128, dtype=np.float32),
    )
    halo = c.MH + c.KH
    in_maps = []
    for core in range(8):
        t0 = core * c.OWN
        lo, hi = t0 - halo, t0 + c.OWN + halo
        xp = np.zeros((c.LK_P, D), np.float32)
        kb = np.full((c.LK_P, 1), -NEGBIG, np.float32)
        a, b = max(lo, 0), min(hi, c.SEQ_P)
        xp[a - lo:b - lo] = x_prompt[0, a:b]
        kb[a - lo:b - lo] = 0.0
        hval = np.zeros((128, 2), np.float32)
        hval[:, 0] = 1.0 if t0 - 1 >= 0 else 0.0
        hval[:, 1] = 1.0 if t0 + c.OWN < c.SEQ_P else 0.0
        m = dict(shared)
        m.update(xp=xp, kbp=kb, xs=np.ascontiguousarray(x_sample[core]), kbs=np.zeros((c.LK_S, 1), np.float32), hval=hval,
                 memp=np.ascontiguousarray(np.asarray(inputs["mem_prompt"], np.float32)[0]),
                 mems=np.ascontiguousarray(np.asarray(inputs["mem_sample"], np.float32)[core]))
        in_maps.append(m)
    res = run_bass_kernel_spmd(nc, in_maps, core_ids=list(range(8)))
    y_prompt = np.concatenate([res.results[i]["yp"] for i in range(8)], axis=0)[None]
    y_sample = np.stack([res.results[i]["ys"] for i in range(8)], axis=0)
    return (y_prompt.astype(np.float32), y_sample.astype(np.float32))


def kernel(**inputs):
    return run(Cfg(), inputs)
```

```python
import contextlib
import math
import numpy as np
import concourse.bass as bass
import concourse.mybir as mybir
from concourse.bass_utils import run_bass_kernel_spmd

F32 = mybir.dt.float32
BF16 = mybir.dt.bfloat16
AF = mybir.ActivationFunctionType
ALU = mybir.AluOpType

SAME_ENGINE_SYNC = True
EPS = 1e-6
NEGBIG = 30000.0
DEN_MERGE = True


class _Op:
    __slots__ = ("eng", "fn", "deps", "is_dma", "dma_key", "sig", "val", "idx")

    def __init__(self, eng, fn, is_dma, dma_key):
        self.eng = eng
        self.fn = fn
        self.deps = []
        self.is_dma = is_dma
        self.dma_key = dma_key
        self.sig = False
        self.val = 0
        self.idx = 0


class Prog:
    ENGS = ("pe", "act", "dve", "pool", "sp")

    def __init__(self, nc):
        self.nc = nc
        self.ops = {e: [] for e in self.ENGS}
        self.last_w = {}
        self.readers = {}
        self.n_ops = 0
        self.last_eng = {}
        self.last_dma = {}

    def op(self, eng, fn, reads=(), writes=(), dma=None, extra_deps=()):
        o = _Op(eng, fn, dma is not None, dma)
        o.idx = self.n_ops
        self.n_ops += 1
        deps = {}
        for t in reads:
            w = self.last_w.get(t)
            if w is not None:
                deps[id(w)] = w
        for t in writes:
            w = self.last_w.get(t)
            if w is not None:
                deps[id(w)] = w
            rs = self.readers.get(t)
            if rs:
                for r in rs.values():
                    deps[id(r)] = r
        for d in extra_deps:
            deps[id(d)] = d
        o.deps = list(deps.values())
        for t in reads:
            rs = self.readers.setdefault(t, {})
            k = ("dma", o.dma_key) if o.is_dma else o.eng
            rs[k] = o
        for t in writes:
            self.last_w[t] = o
            self.readers[t] = {}
        self.ops[eng].append(o)
        if o.is_dma:
            self.last_dma[o.dma_key] = o
        else:
            self.last_eng[eng] = o
        return o

    def barrier(self):
        deps = list(self.last_eng.values()) + list(self.last_dma.values())
        for e in self.ENGS:
            self.op(e, lambda eng: eng.nop(), extra_deps=[d for d in deps])

    def emit(self, stack):
        nc = self.nc
        for e in self.ENGS:
            for o in self.ops[e]:
                for d in o.deps:
                    if d.is_dma:
                        d.sig = True
                    elif d.eng == o.eng and not o.is_dma:
                        if o.eng != "pe" and SAME_ENGINE_SYNC:
                            d.sig = True
                    else:
                        d.sig = True
        sems = {}
        cnt = {}
        for e in self.ENGS:
            for o in self.ops[e]:
                if o.is_dma:
                    k = ("dma", o.dma_key)
                    cnt[k] = cnt.get(k, 0) + 16
                    o.val = cnt[k]
                    o.sig = True
                elif o.sig:
                    k = ("eng", e)
                    cnt[k] = cnt.get(k, 0) + 1
                    o.val = cnt[k]
        final_dma = {k: v for k, v in cnt.items() if k[0] == "dma"}
        for i, k in enumerate(cnt):
            sems[k] = stack.enter_context(nc.semaphore("s%d" % i))
        engmap = {"pe": "tensor", "act": "scalar", "dve": "vector", "pool": "gpsimd", "sp": "sync"}
        block = stack.enter_context(nc.Block())
        prog = self

        def make(e):
            def body(eng):
                known = {}
                for o in prog.ops[e]:
                    need = {}
                    for d in o.deps:
                        if d.is_dma:
                            k = ("dma", d.dma_key)
                        else:
                            if d.eng == e and not o.is_dma:
                                if e == "pe" or not SAME_ENGINE_SYNC:
                                    continue
                            k = ("eng", d.eng)
                        if d.val > need.get(k, 0):
                            need[k] = d.val
                    for k, v in need.items():
                        if known.get(k, 0) >= v:
                            continue
                        known[k] = v
                        eng.wait_ge(sems[k], v)
                    ins = o.fn(eng)
                    if o.is_dma:
                        ins.then_inc(sems[("dma", o.dma_key)], 16)
                    elif o.sig:
                        ins.then_inc(sems[("eng", e)], 1)
                if e == "sp":
                    for k, v in final_dma.items():
                        if known.get(k, 0) < v:
                            eng.wait_ge(sems[k], v)
            return body

        for e in self.ENGS:
            getattr(block, engmap[e])(make(e))


class Cfg:
    def __init__(self, D=4096, AH=16, BH=16, XH=4, DFF=11008, FFG=8, TM=3, ARENA=28880):
        self.D, self.AH, self.BH, self.XH, self.DFF, self.FFG, self.TM = D, AH, BH, XH, DFF, FFG, TM
        self.KC = D // 128
        self.AW, self.BW, self.XW = AH * 128, BH * 128, XH * 128
        self.INW = 2 * self.AW + 3 * self.BW
        self.GF = DFF // 128
        self.NMEM = 256
        self.SEQ_P = 16384
        self.SEQ_S = 2048
        self.OWN = 2048
        self.MH = 128
        self.KH = 1024
        self.LM_P = self.OWN + 2 * self.MH
        self.LK_P = self.LM_P + 2 * self.KH
        self.LM_S = self.SEQ_S
        self.LK_S = self.SEQ_S
        self.ARENA = ARENA


DIL = (1, 4, 16)


class WT:
    def __init__(self, K, name, src, c0, c1, mode):
        self.K = K
        self.name = name
        self.KC = src.shape[0] // 128
        W = c1 - c0
        self.cw = min(256 if mode == "s" else 512, W)
        self.kcp = min(self.KC, 4096 // self.cw)
        self.ncb = W // self.cw
        self.nkp = -(-self.KC // self.kcp)
        self.c0 = c0
        self.dram = K.nc.dram_tensor("wd_" + name, [self.ncb, self.nkp, 128, self.kcp, self.cw], BF16).ap()
        self.srcv = src.rearrange("(kc p) n -> p kc n", p=128)

    def kcs(self, kp):
        return min(self.kcp, self.KC - kp * self.kcp)

    def precast_ops(self):
        return [(self, cb, kp) for cb in range(self.ncb) for kp in range(self.nkp)]

    def precast(self, cb, kp):
        K = self.K
        n = self.kcs(kp)
        dst = self.dram[cb, kp][:, 0:n, :]
        s = self.srcv[:, kp * self.kcp:kp * self.kcp + n, self.c0 + cb * self.cw: self.c0 + (cb + 1) * self.cw]
        key = K.next_cast_key()
        K.dma("pool", dst, s, [], [("wd", self.name, cb, kp), ("castkey", key)], key)

    def load(self, cb, kp):
        K = self.K
        i = K.wslot_next
        K.wslot_next = (i + 1) % K.NSLOT
        n = self.kcs(kp)
        view = K.wslots[i][:, 0:n * self.cw].rearrange("p (a b) -> p a b", b=self.cw)
        K.dma("sp", view, self.dram[cb, kp][:, 0:n, :], [("wd", self.name, cb, kp)], [("ws", i)], ("ws", i))
        return view, ("ws", i)


class Carve:
    def __init__(self, K):
        self.K = K
        self.off = 0

    def f32(self, shape):
        n = int(np.prod(shape[1:]))
        v = self.K.arena[:, self.off:self.off + n]
        self.off += n
        assert self.off <= self.K.arena_words, (self.off, self.K.arena_words)
        return self._shape(v, shape)

    def bf16(self, shape):
        n = int(np.prod(shape[1:]))
        w = (n + 1) // 2
        v = self.K.arena[:, self.off:self.off + w].bitcast(BF16)[:, 0:n]
        self.off += w
        assert self.off <= self.K.arena_words, (self.off, self.K.arena_words)
        return self._shape(v, shape)

    @staticmethod
    def _shape(v, shape):
        if len(shape) == 2:
            return v
        return v.rearrange("p (a b) -> p a b", b=shape[2])


class Kern:
    NSLOT = 6

    def __init__(self, cfg, arena_words=None):
        self.c = cfg
        self.nc = bass.Bass("TRN2", target_bir_lowering=False)
        self.P = Prog(self.nc)
        self.wslot_next = 0
        self.cast_i = 0
        self.rot = {}
        self.arena_words = arena_words or cfg.ARENA
        self.max_off = 0

    def next_cast_key(self):
        self.cast_i += 1
        return ("cast", self.cast_i % 8)

    def rr(self, name, n):
        i = self.rot.get(name, 0)
        self.rot[name] = (i + 1) % n
        return i

    def dma(self, q, out, in_, r, w, key):
        return self.P.op(q, lambda e: e.dma_start(out=out, in_=in_), reads=r, writes=w, dma=key)

    def mm(self, out, lhsT, rhs, start, stop, r, w, **kw):
        return self.P.op("pe", lambda e: e.matmul(out, lhsT=lhsT, rhs=rhs, start=start, stop=stop, **kw), reads=r, writes=w)

    def tr(self, out, in_, r, w):
        ident = self.ident[:]
        return self.P.op("pe", lambda e: e.transpose(out, in_, ident), reads=list(r) + ["ident"], writes=w)

    def act(self, out, in_, func, r, w, **kw):
        return self.P.op("act", lambda e: e.activation(out=out, in_=in_, func=func, **kw), reads=r, writes=w)

    def acopy(self, out, in_, r, w):
        return self.P.op("act", lambda e: e.copy(out=out, in_=in_), reads=r, writes=w)

    def vcopy(self, out, in_, r, w, eng="dve"):
        return self.P.op(eng, lambda e: e.tensor_copy(out=out, in_=in_), reads=r, writes=w)

    def tt(self, out, in0, in1, op, r, w, eng="dve"):
        return self.P.op(eng, lambda e: e.tensor_tensor(out=out, in0=in0, in1=in1, op=op), reads=r, writes=w)

    def ts(self, out, in0, s1, s2, op0, op1, r, w, eng="dve"):
        if s2 is None:
            return self.P.op(eng, lambda e: e.tensor_scalar(out=out, in0=in0, scalar1=s1, scalar2=None, op0=op0), reads=r, writes=w)
        return self.P.op(eng, lambda e: e.tensor_scalar(out=out, in0=in0, scalar1=s1, scalar2=s2, op0=op0, op1=op1), reads=r, writes=w)

    def stt(self, out, in0, scalar, in1, op0, op1, r, w, eng="dve"):
        return self.P.op(eng, lambda e: e.scalar_tensor_tensor(out=out, in0=in0, scalar=scalar, in1=in1, op0=op0, op1=op1), reads=r, writes=w)

    def rcp(self, out, in_, r, w):
        return self.P.op("dve", lambda e: e.reciprocal(out=out, in_=in_), reads=r, writes=w)

    def mset(self, out, val, w, eng="dve"):
        return self.P.op(eng, lambda e: e.memset(out, val), writes=w)

    def build(self):
        c, nc, P = self.c, self.nc, self.P
        D, KC = c.D, c.KC
        inp = lambda n, s, d=F32: nc.dram_tensor(n, list(s), d, kind="ExternalInput").ap()
        self.xp = inp("xp", [c.LK_P, D])
        self.xs = inp("xs", [c.LK_S, D])
        self.kbp = inp("kbp", [c.LK_P, 1])
        self.kbs = inp("kbs", [c.LK_S, 1])
        self.hval = inp("hval", [128, 2])
        self.memp = inp("memp", [c.NMEM, D])
        self.mems = inp("mems", [c.NMEM, D])
        self.w_in = inp("w_in", [D, c.INW])
        self.w_out = inp("w_out", [D, D])
        self.w_xq = inp("w_xq", [D, c.XW])
        self.w_xkv = inp("w_xkv", [D, 2 * c.XW])
        self.w_xo = inp("w_xo", [c.XW, D])
        self.w_up = inp("w_up", [D, 2 * c.DFF])
        self.w_down = inp("w_down", [c.DFF, D])
        self.gvec = {n: inp(n, [128, KC]) for n in ("norm_mix_g", "norm_x_g", "mem_norm_g", "norm_ffn_g")}
        self.final_g = inp("final_g", [D])
        self.gvecD = {n: inp(n + "_v", [D]) for n in ("norm_mix_g", "norm_x_g", "mem_norm_g", "norm_ffn_g")}
        self.sg_ln_g = inp("sg_ln_g", [c.AW])
        self.sg_ln_b = inp("sg_ln_b", [c.AW])
        self.grp_a_g = inp("grp_a_g", [c.AW])
        self.grp_b_g = inp("grp_b_g", [128, c.BH])
        self.sg_wT = inp("sg_wT", [128, c.AH, 128])
        self.sg_bT = inp("sg_bT", [128, c.AH])
        self.conv_w = inp("conv_w", [128, 3, 2 * c.GF])
        self.conv_b = inp("conv_b", [128, 2 * c.GF])
        self.ndm = inp("ndm", [128, 256])
        self.identm = inp("identm", [128, 128])
        self.yp = nc.dram_tensor("yp", [c.OWN, D], F32, kind="ExternalOutput").ap()
        self.ys = nc.dram_tensor("ys", [c.SEQ_S, D], F32, kind="ExternalOutput").ap()
        dt = lambda n, s, d: nc.dram_tensor(n, list(s), d).ap()
        self.seg = {}
        for sname, LK, LM, H, xin, kb, mem, yout, mh in (
            ("p", c.LK_P, c.LM_P, c.KH, self.xp, self.kbp, self.memp, self.yp, c.MH),
            ("s", c.LK_S, c.LM_S, 0, self.xs, self.kbs, self.mems, self.ys, 0),
        ):
            self.seg[sname] = dict(
                name=sname, LK=LK, LM=LM, H=H, x=xin, kb=kb, mem=mem, y=yout, MH=mh,
                Kd=dt("Kd" + sname, [c.BW, LK], BF16), Vd=dt("Vd" + sname, [c.BW, LK], BF16),
                Hd=dt("Hd" + sname, [D, LM], BF16), X2d=dt("X2d" + sname, [LM, D], F32))
        A2 = 2 * c.AW
        self.W = dict(
            k=WT(self, "k", self.w_in, A2 + c.BW, A2 + 2 * c.BW, "s"),
            vB=WT(self, "vB", self.w_in, A2 + 2 * c.BW, A2 + 3 * c.BW, "s"),
            xk=WT(self, "xk", self.w_xkv, 0, c.XW, "s"),
            xv=WT(self, "xv", self.w_xkv, c.XW, 2 * c.XW, "m"),
            u=WT(self, "u", self.w_in, 0, c.AW, "m"),
            vA=WT(self, "vA", self.w_in, c.AW, A2, "m"),
            q=WT(self, "q", self.w_in, A2, A2 + c.BW, "s"),
            o=WT(self, "o", self.w_out, 0, D, "m"),
            xq=WT(self, "xq", self.w_xq, 0, c.XW, "s"),
            xo=WT(self, "xo", self.w_xo, 0, D, "m"),
            up=WT(self, "up", self.w_up, 0, 2 * c.DFF, "s"),
            dn=WT(self, "dn", self.w_down, 0, D, "m"),
        )
        with contextlib.ExitStack() as st:
            self.st = st
            self.alloc()
            self.setup_consts()
            self.precast_q = []
            for n in ("k", "vB", "xk", "xv", "u", "vA", "q", "o", "xq", "xo", "up", "dn"):
                self.precast_q += self.W[n].precast_ops()
            self.drain_precast(len(self.W["k"].precast_ops()) + len(self.W["vB"].precast_ops()))
            for sname in ("p", "s"):
                self.kv_pass(self.seg[sname])
            n_late = len(self.W["up"].precast_ops()) + len(self.W["dn"].precast_ops())
            self.drain_precast(len(self.precast_q) - n_late)
            for sname in ("p", "s"):
                self.mem_kv(self.seg[sname])
                self.mixer_pass(self.seg[sname])
            self.drain_precast(10 ** 9)
            for sname in ("p", "s"):
                self.ffn_pass(self.seg[sname])
            P.emit(st)
        return nc

    def drain_precast(self, n):
        while n > 0 and self.precast_q:
            wt, cb, kp = self.precast_q.pop(0)
            wt.precast(cb, kp)
            n -= 1

    def carve(self):
        if hasattr(self, "_cv"):
            self.max_off = max(self.max_off, self._cv.off)
        self._cv = Carve(self)
        return self._cv

    def alloc(self):
        c, nc, st = self.c, self.nc, self.st
        D, KC = c.D, c.KC
        sb = lambda n, s, d: st.enter_context(nc.sbuf_tensor(n, list(s), d))
        self.wslots = [sb("ws%d" % i, [128, 4096], BF16) for i in range(self.NSLOT)]
        self.hA = sb("hA", [128, KC, 516], BF16)
        self.hAtok = "hA"
        self.ident = sb("ident", [128, 128], BF16)
        self.ones = sb("ones", [128, 128], BF16)
        self.zeros = sb("zeros", [128, 512], BF16)
        self.nd = sb("nd", [128, 256], F32)
        self.epsc = sb("epsc", [128, 1], F32)
        self.gT = {n: sb("gT_" + n, [128, KC], F32) for n in self.gvec}
        self.grpBT = sb("grpBT", [128, c.BH], F32)
        self.sgwT = sb("sgwT", [128, c.AH, 128], BF16)
        self.sgbT = sb("sgbT", [128, c.AH], F32)
        self.rsW = sb("rsW", [128, c.AH], F32)
        self.cw = sb("convw", [128, 3, 2 * c.GF], F32)
        self.cb = sb("convb", [128, 2 * c.GF], F32)
        self.hv = sb("hv", [128, 2], F32)
        self.stat = sb("stat", [128, 64 + 32 * 4], F32)
        self.Kx = sb("Kx", [128, c.XH, c.NMEM], BF16)
        self.Vx = sb("Vx", [128, 2, c.XW], BF16)
        self.arena = sb("arena", [128, self.arena_words], F32)
        ps = lambda n, s, d: st.enter_context(nc.psum_tensor(n, list(s), d))
        self.pb = [ps("pb%d" % i, [128, 512], F32) for i in range(6)]
        self.pt = [ps("pt%d" % i, [128, 1024], BF16) for i in range(2)]

    def vt_elems(self, T):
        tot = 0
        for d in DIL:
            nk = T // d + 128
            tot += (-(-nk // 128)) * d * 128
        return tot

    def setup_consts(self):
        c = self.c
        self.dma("pool", self.ident[:], self.identm, [], ["ident"], "k_ident")
        self.dma("pool", self.nd[:], self.ndm, [], ["nd"], "k_nd")
        self.dma("pool", self.sgwT[:], self.sg_wT, [], ["sgwT"], "k_sgwT")
        self.dma("pool", self.sgbT[:], self.sg_bT, [], ["sgbT"], "k_sgbT")
        self.dma("pool", self.hv[:], self.hval, [], ["hv"], "k_hv")
        for i, (n, t) in enumerate(self.gT.items()):
            self.dma("pool", t[:], self.gvec[n], [], ["gT_" + n], "k_gT_" + n)
        self.dma("pool", self.grpBT[:], self.grp_b_g, [], ["grpBT"], "k_grpBT")
        self.dma("pool", self.cw[:], self.conv_w, [], ["convw"], "k_convw")
        self.dma("pool", self.cb[:], self.conv_b, [], ["convb"], "k_convb")
        self.mset(self.ones[:], 1.0, ["ones"])
        self.mset(self.zeros[:], 0.0, ["zeros"])
        self.mset(self.epsc[:], EPS, ["epsc"])
        for g in range(c.AH):
            self.mm(self.pb[4][:, g:g + 1], self.sgwT[:, g, :], self.ones[:, 0:1], True, True, ["sgwT", "ones"], [("pb", 4)])
        self.vcopy(self.rsW[:], self.pb[4][:, 0:c.AH], [("pb", 4)], ["rsW"])

    def load_grep(self, buf, gname):
        self.dma("pool", buf, self.gvecD[gname].partition_broadcast(128), [], ["grep"], ("cA", "grep"))

    def make_h(self, src_tile, src_tok, xs_buf, xs_tok, grep, col0, stat_col=0, h=None, htok=None):
        c = self.c
        D, KC = c.D, c.KC
        if h is None:
            h, htok = self.hA, self.hAtok
        ssq = self.stat[:, stat_col:stat_col + 1]
        rt = self.stat[:, stat_col + 1:stat_col + 2]
        rstd = self.stat[:, stat_col + 2:stat_col + 3]
        s0, s1, s2 = ("st", stat_col), ("st", stat_col + 1), ("st", stat_col + 2)
        self.act(xs_buf, src_tile, AF.Square, [src_tok], [xs_tok, s0], accum_out=ssq)
        self.act(rt, ssq, AF.Sqrt, [s0, "epsc"], [s1], bias=self.epsc[:, 0:1], scale=1.0 / D)
        self.rcp(rstd, rt, [s1], [s2])
        self.stt(xs_buf, src_tile, rstd, grep, ALU.mult, ALU.mult, [src_tok, s2, "grep"], [xs_tok])
        G = min(8, KC)
        for kg in range(KC // G):
            b = self.rr("pt", 2)
            for j in range(G):
                kc = kg * G + j
                self.tr(self.pt[b][:, j * 128:(j + 1) * 128], xs_buf[:, kc * 128:(kc + 1) * 128], [xs_tok], [("pt", b)])
            src = self.pt[b][:, 0:G * 128].rearrange("p (a b) -> p a b", b=128)
            dst = h[:, kg * G:(kg + 1) * G, col0:col0 + 128]
            self.vcopy(dst, src, [("pt", b)], [htok])

    def linear_fm(self, wt, h, htok, T, evac, tcol0=0):
        ngrp = wt.cw // 128
        for cb in range(wt.ncb):
            banks = [self.rr("pbl", 4) for _ in range(ngrp)]
            for kp in range(wt.nkp):
                view, tok = wt.load(cb, kp)
                for g in range(ngrp):
                    for kc in range(wt.kcs(kp)):
                        kk = kp * wt.kcp + kc
                        self.mm(self.pb[banks[g]][:, 0:T], view[:, kc, g * 128:(g + 1) * 128], h[:, kk, tcol0:tcol0 + T],
                                kk == 0, kk == wt.KC - 1, [tok, htok], [("pb", banks[g])])
            for g in range(ngrp):
                evac(cb * ngrp + g, self.pb[banks[g]][:, 0:T], ("pb", banks[g]))

    def linear_tm(self, wt, h, htok, ntt, evac):
        for cb in range(wt.ncb):
            for kp in range(wt.nkp):
                view, tok = wt.load(cb, kp)
                for tt in range(ntt):
                    for kc in range(wt.kcs(kp)):
                        kk = kp * wt.kcp + kc
                        self.mm(self.pb[tt][:, 0:wt.cw], h[:, kk, tt * 128:(tt + 1) * 128], view[:, kc, :],
                                kk == 0, kk == wt.KC - 1, [tok, htok], [("pb", tt)])
                    if kp == wt.nkp - 1:
                        evac(cb, tt, self.pb[tt][:, 0:wt.cw], ("pb", tt))

    def kv_pass(self, sg):
        c, P = self.c, self.P
        D = c.D
        P.barrier()
        cv = self.carve()
        xt = [cv.f32([128, D]) for _ in range(4)]
        xsb = [cv.bf16([128, D]) for _ in range(2)]
        kst = [cv.bf16([128, 512]) for _ in range(4)]
        vst = [cv.bf16([128, 512]) for _ in range(4)]
        grep = cv.f32([128, D])
        self.load_grep(grep, "norm_mix_g")
        ntiles = sg["LK"] // 128
        hA, htok = self.hA, self.hAtok
        wv = self.W["vB"]
        sn = sg["name"]
        blocks = []
        t0 = 0
        while t0 < ntiles:
            ntt = min(4, ntiles - t0)
            blocks.append((t0, ntt))
            t0 += ntt

        def issue_loads(blk):
            t0, ntt = blk
            for tt in range(ntt):
                k = t0 * 128 + tt * 128
                self.dma("pool", xt[tt], sg["x"][k:k + 128, :], [], [("xt", tt)], ("xt", tt))
        issue_loads(blocks[0])
        for bi, (t0, ntt) in enumerate(blocks):
            T = ntt * 128
            k0 = t0 * 128
            for tt in range(ntt):
                i = tt % 2
                self.make_h(xt[tt], ("xt", tt), xsb[i], ("xs", i), grep, tt * 128, stat_col=4 * i)
            if bi + 1 < len(blocks):
                issue_loads(blocks[bi + 1])
            self.drain_precast(8)

            def evk(g, pap, ptok, T=T, k0=k0):
                i = self.rr("kst", 4)
                self.acopy(kst[i][:, 0:T], pap, [ptok], [("kst", i)])
                self.dma("pool", sg["Kd"][g * 128:(g + 1) * 128, k0:k0 + T], kst[i][:, 0:T], [("kst", i)], [("Kd", sn)], ("kst", i))
            self.linear_fm(self.W["k"], hA, htok, T, evk)

            def evv(g, pap, ptok, T=T, k0=k0):
                i = self.rr("vst", 4)
                self.vcopy(vst[i][:, 0:T], pap, [ptok], [("vst", i)])
                self.dma("pool", sg["Vd"][g * 128:(g + 1) * 128, k0:k0 + T], vst[i][:, 0:T], [("vst", i)], [("Vd", sn)], ("vst", i))
            self.linear_fm(wv, hA, htok, T, evv)

    def mem_kv(self, sg):
        c, P = self.c, self.P
        D = c.D
        P.barrier()
        cv = self.carve()
        xt = [cv.f32([128, D]) for _ in range(2)]
        xsb = [cv.bf16([128, D]) for _ in range(2)]
        hA, htok = self.hA, self.hAtok
        grep = cv.f32([128, D])
        self.load_grep(grep, "mem_norm_g")
        for tt in range(2):
            self.dma("pool", xt[tt], sg["mem"][tt * 128:(tt + 1) * 128, :], [], [("xt", tt)], ("xt", tt))
            self.make_h(xt[tt], ("xt", tt), xsb[tt], ("xs", tt), grep, tt * 128, stat_col=4 * tt)

        def evk(g, pap, ptok):
            self.acopy(self.Kx[:, g, :], pap, [ptok], ["Kx"])
        self.linear_fm(self.W["xk"], hA, htok, c.NMEM, evk)
        wv = self.W["xv"]

        def evv(cb, tt, pap, ptok):
            self.acopy(self.Vx[:, tt, cb * wv.cw:(cb + 1) * wv.cw], pap, [ptok], ["Vx"])
        self.linear_tm(wv, hA, htok, 2, evv)

    def mixer_pass(self, sg):
        c = self.c
        ntiles = sg["LM"] // 128
        t0 = 0
        while t0 < ntiles:
            ntt = min(c.TM, ntiles - t0)
            self.mixer_block(sg, t0 * 128, ntt)
            t0 += ntt

    def mixer_block(self, sg, tau0, ntt):
        c, P = self.c, self.P
        D, KC = c.D, c.KC
        T = ntt * 128
        TMX = c.TM * 128
        H = sg["H"]
        sn = sg["name"]
        hA, hAtok = self.hA, self.hAtok
        scale = 128 ** -0.5
        st = self.stat
        P.barrier()
        cv = self.carve()
        hB = cv.bf16([128, KC, TMX])
        hBtok = "hB"
        base = cv.off
        xt = [cv.f32([128, D]) for _ in range(2)]
        xsb = [cv.bf16([128, D]) for _ in range(2)]
        xrow = lambda tt: sg["x"][H + tau0 + tt * 128: H + tau0 + (tt + 1) * 128, :]
        grepN = cv.f32([128, D])
        self.load_grep(grepN, "norm_mix_g")
        for tt in range(ntt):
            i = tt % 2
            self.dma("pool", xt[i], xrow(tt), [], [("xt", i)], ("xt", i))
            self.make_h(xt[i], ("xt", i), xsb[i], ("xs", i), grepN, tt * 128, stat_col=4 * i)
        P.barrier()
        self.max_off = max(self.max_off, cv.off)
        cv.off = base
        AW, AH = c.AW, c.AH
        u = cv.bf16([128, c.TM, AW])
        vA = cv.bf16([128, c.TM, AW])
        lnG = cv.f32([128, AW])
        lnB = cv.f32([128, AW])
        grA = cv.f32([128, AW])
        Cb = cv.f32([128, AW])
        tmpA = [cv.f32([128, 512]) for _ in range(2)]
        aouts = [lnB] + [cv.f32([128, AW]) for _ in range(c.TM - 1)]
        ans = [cv.bf16([128, AW]) for _ in range(c.TM)]
        for dst, src, nm in ((lnG, self.sg_ln_g, "lnG"), (lnB, self.sg_ln_b, "lnB"), (grA, self.grp_a_g, "grA")):
            self.dma("pool", dst, src.partition_broadcast(128), [], [nm], ("cA", nm))
        for g in range(AH):
            self.ts(Cb[:, g * 128:(g + 1) * 128], lnB[:, g * 128:(g + 1) * 128], self.rsW[:, g:g + 1], self.sgbT[:, g:g + 1],
                    ALU.mult, ALU.add, ["lnB", "rsW", "sgbT"], ["Cb"])
        self.drain_precast(11)
        wu, wva = self.W["u"], self.W["vA"]

        def evu(cb, tt, pap, ptok):
            self.act(u[:, tt, cb * wu.cw:(cb + 1) * wu.cw], pap, AF.Gelu, [ptok], [("u", tt)])
        self.linear_tm(wu, hA, hAtok, ntt, evu)

        def evva(cb, tt, pap, ptok):
            self.act(vA[:, tt, cb * wva.cw:(cb + 1) * wva.cw], pap, AF.Gelu, [ptok], [("vA", tt)])
        self.linear_tm(wva, hA, hAtok, ntt, evva)
        bw = min(512, max(64, AW // 2))
        nbn = AW // bw
        S0 = 64
        sc = lambda tt, j: S0 + 32 * tt + j
        for tt in range(ntt):
            for j in range(nbn):
                P.op("dve", (lambda o, i: (lambda e: e.bn_stats(out=o, in_=i)))(st[:, sc(tt, 8 + 6 * j):sc(tt, 14 + 6 * j)], vA[:, tt, j * bw:(j + 1) * bw]),
                     reads=[("vA", tt)], writes=[("bn", tt, j)])
            P.op("dve", (lambda o, i: (lambda e: e.bn_aggr(out=o, in_=i)))(st[:, sc(tt, 0):sc(tt, 2)],
                                                                         st[:, sc(tt, 8):sc(tt, 8 + 6 * nbn)].rearrange("p (a b) -> p a b", b=6)),
                 reads=[("bn", tt, j) for j in range(nbn)], writes=[("mv", tt)])
        for tt in range(ntt):
            self.act(st[:, sc(tt, 2):sc(tt, 3)], st[:, sc(tt, 1):sc(tt, 2)], AF.Sqrt, [("mv", tt), "epsc"], [("lnr", tt)], bias=self.epsc[:, 0:1], scale=1.0)
        for tt in range(ntt):
            self.rcp(st[:, sc(tt, 3):sc(tt, 4)], st[:, sc(tt, 2):sc(tt, 3)], [("lnr", tt)], [("lnr2", tt)])
        for tt in range(ntt):
            self.ts(vA[:, tt, :], vA[:, tt, :], st[:, sc(tt, 0):sc(tt, 1)], st[:, sc(tt, 3):sc(tt, 4)], ALU.subtract, ALU.mult,
                    [("vA", tt), ("mv", tt), ("lnr2", tt)], [("vA", tt)])
        hpb = min(4, AH)
        for tt in range(ntt):
            aout = aouts[tt]
            for g0 in range(0, AH, hpb):
                bk = 4 + self.rr("pbm", 2)
                for g in range(g0, g0 + hpb):
                    self.mm(self.pb[bk][:, (g - g0) * 128:(g - g0 + 1) * 128], self.sgwT[:, g, :], vA[:, tt, g * 128:(g + 1) * 128],
                            True, True, ["sgwT", ("vA", tt)], [("pb", bk)])
                cols = slice(g0 * 128, (g0 + hpb) * 128)
                n = hpb * 128
                ti = self.rr("tmpA", 2)
                tA, ttok = tmpA[ti], ("tmpA", ti)
                self.tt(tA[:, 0:n], self.pb[bk][:, 0:n], lnG[:, cols], ALU.mult, [("pb", bk), "lnG"], [ttok])
                self.tt(tA[:, 0:n], tA[:, 0:n], Cb[:, cols], ALU.add, [ttok, "Cb"], [ttok], eng="pool")
                wtoks = [("aout", tt)] + (["lnB"] if tt == 0 else [])
                self.tt(aout[:, cols], tA[:, 0:n], u[:, tt, cols], ALU.mult, [ttok, ("u", tt)], wtoks)
        for tt in range(ntt):
            self.act(ans[tt][:], aouts[tt][:], AF.Square, [("aout", tt)], [("an", tt), ("ssA", tt)], accum_out=st[:, sc(tt, 4):sc(tt, 5)])
        for tt in range(ntt):
            self.act(st[:, sc(tt, 5):sc(tt, 6)], st[:, sc(tt, 4):sc(tt, 5)], AF.Sqrt, [("ssA", tt), "epsc"], [("ssA2", tt)],
                     bias=self.epsc[:, 0:1], scale=1.0 / AW)
        for tt in range(ntt):
            self.rcp(st[:, sc(tt, 6):sc(tt, 7)], st[:, sc(tt, 5):sc(tt, 6)], [("ssA2", tt)], [("ssA3", tt)])
        for tt in range(ntt):
            self.stt(ans[tt][:], aouts[tt][:], st[:, sc(tt, 6):sc(tt, 7)], grA[:], ALU.mult, ALU.mult,
                     [("aout", tt), ("ssA3", tt), "grA"], [("an", tt)])
        G = min(8, AH)
        for tt in range(ntt):
            for kg in range(AH // G):
                b = self.rr("pt", 2)
                for j in range(G):
                    g = kg * G + j
                    self.tr(self.pt[b][:, j * 128:(j + 1) * 128], ans[tt][:, g * 128:(g + 1) * 128], [("an", tt)], [("pt", b)])
                self.acopy(hB[:, kg * G:(kg + 1) * G, tt * 128:(tt + 1) * 128],
                           self.pt[b][:, 0:G * 128].rearrange("p (a b) -> p a b", b=128), [("pt", b)], [hBtok])
        P.barrier()
        self.max_off = max(self.max_off, cv.off)
        cv.off = base
        BH = c.BH
        q = cv.bf16([128, BH, TMX])
        WK = TMX + 2 * c.KH
        kwin = [cv.bf16([128, WK]) for _ in range(2)]
        vwin = [cv.bf16([128, WK]) for _ in range(2)]
        ch = self.attn_chunks(sg, tau0, T)
        nve = self.vt_elems(TMX)
        assert ch["vtot"] <= nve
        vts = [cv.bf16([128, nve]) for _ in range(2)]
        bo = cv.bf16([128, BH, TMX])
        kbt = cv.f32([128, 64])
        ndk_off = {}
        _o = 0
        for it in ch["bitems"]:
            ndk_off[(it[0], it[1])] = _o
            _o += it[0] * it[4]
        ndk = cv.f32([128, max(_o, 1)])
        NS = max(TMX, 256)
        ptS = [cv.bf16([128, NS]) for _ in range(3)]
        tmpS = [cv.f32([128, NS]) for _ in range(3)]
        rden = cv.f32([128, TMX])
        sq = cv.bf16([128, TMX])
        rstdB = cv.f32([128, TMX])

        def evq(g, pap, ptok):
            self.acopy(q[:, g, 0:T], pap, [ptok], [("q", g)])
        self.linear_fm(self.W["q"], hA, hAtok, T, evq)
        assert ch["ncol"] <= 48
        for (d, a, nk, ci) in ch["vload"]:
            src = bass.AP(sg["kb"].tensor, sg["kb"].offset + d * a, [[d, nk], [1, d]])
            self.dma("pool", kbt[0:nk, ci:ci + d], src, [], ["kbt"], ("kbt", 0))
        for (d, a, nk, qa, nq, ci) in ch["bitems"]:
            J0 = qa - a + 64
            o0 = ndk_off[(d, a)]
            if d > 1:
                self.tt(ndk[0:nk, o0:o0 + d * nq].rearrange("p (r q) -> p r q", q=nq),
                        self.nd[0:nk, J0:J0 + nq].unsqueeze(1).to_broadcast([nk, d, nq]),
                        kbt[0:nk, ci:ci + d].unsqueeze(2).to_broadcast([nk, d, nq]), ALU.add, ["nd", "kbt"], ["ndk"])
            else:
                self.ts(ndk[0:nk, o0:o0 + nq], self.nd[0:nk, J0:J0 + nq], kbt[0:nk, ci:ci + 1], None, ALU.add, None, ["nd", "kbt"], ["ndk"])
        wlo, wn = ch["wlo"], ch["wn"]
        def load_head(h):
            bi = h % 2
            self.dma("pool", kwin[bi][:, 0:wn], sg["Kd"][h * 128:(h + 1) * 128, wlo:wlo + wn], [("Kd", sn)], [("kwin", bi)], ("kwin", bi))
            self.dma("pool", vwin[bi][:, 0:wn], sg["Vd"][h * 128:(h + 1) * 128, wlo:wlo + wn], [("Vd", sn)], [("vwin", bi)], ("vwin", bi))

        def tr_groups(h):
            bi = h % 2
            tiles = []
            for (d, a, nk, ci) in ch["vload"]:
                for rho in range(d):
                    tiles.append((d, a, nk, rho, ch["voff"][(d, a)] + rho * 128))
            out = []
            ti = 0
            while ti < len(tiles):
                nk0 = tiles[ti][2]
                grp = [tiles[ti]]
                while len(grp) < 8 and ti + len(grp) < len(tiles) and tiles[ti + len(grp)][2] == nk0:
                    grp.append(tiles[ti + len(grp)])

                def emit(grp=grp, nk0=nk0, bi=bi):
                    b = self.rr("pt", 2)
                    for j, (d, a, nk, rho, off) in enumerate(grp):
                        c0 = rho + d * a - wlo
                        self.tr(self.pt[b][0:nk, j * 128:(j + 1) * 128], vwin[bi][:, c0:c0 + d * (nk - 1) + 1:d], [("vwin", bi)], [("pt", b)])
                    off0 = grp[0][4]
                    self.acopy(vts[bi][0:nk0, off0:off0 + len(grp) * 128], self.pt[b][0:nk0, 0:len(grp) * 128], [("pt", b)], [("vt", bi)])
                out.append(emit)
                ti += len(grp)
            return out

        def tr_head(h):
            for g in tr_groups(h):
                g()
        load_head(0)
        tr_head(0)
        for h in range(BH):
            bi = h % 2
            if h + 1 < BH:
                load_head(h + 1)
            slope = 2.0 ** (-8.0 * (h + 1) / BH)
            pend = []
            zeroed = [False]
            nxt = tr_groups(h + 1) if h + 1 < BH else []

            def zero_init():
                if not zeroed[0]:
                    zeroed[0] = True
                    for bk in (4, 5):
                        self.mm(self.pb[bk][:, 0:T], self.zeros[:, 0:128], self.zeros[:, 0:T], True, False, ["zeros"], [("pb", bk)])

            def pv(it, si, bi=bi):
                (d, a, nk, qa, nq, ci) = it
                zero_init()
                for rho in range(d):
                    off = ch["voff"][(d, a)] + rho * 128
                    qcol0 = rho + d * qa - H - tau0
                    cols = slice(qcol0, qcol0 + d * (nq - 1) + 1, d)
                    pslice = ptS[si][0:nk, rho * nq:(rho + 1) * nq]
                    self.mm(self.pb[4][:, cols], vts[bi][0:nk, off:off + 128], pslice, False, False,
                            [("vt", bi), ("ptS", si)], [("pb", 4)], skip_group_check=True)
                    if not (DEN_MERGE and d > 1):
                        self.mm(self.pb[5][:, cols], self.ones[0:nk, :], pslice, False, False,
                                ["ones", ("ptS", si)], [("pb", 5)], skip_group_check=True)
                if DEN_MERGE and d > 1:
                    base = d * qa - H - tau0
                    self.mm(self.pb[5][:, base:base + d * nq].rearrange("p (j r) -> p r j", r=d), self.ones[0:nk, :],
                            ptS[si][0:nk, 0:d * nq].rearrange("p (r q) -> p r q", q=nq), False, False,
                            ["ones", ("ptS", si)], [("pb", 5)], skip_group_check=True)
            for n_it, it in enumerate(ch["bitems"]):
                (d, a, nk, qa, nq, ci) = it
                si = n_it % 3
                n = d * nq
                assert n <= NS
                for rho in range(d):
                    kcol0 = rho + d * a - wlo
                    kcols = slice(kcol0, kcol0 + d * (nk - 1) + 1, d)
                    qcol0 = rho + d * qa - H - tau0
                    qcols = slice(qcol0, qcol0 + d * (nq - 1) + 1, d)
                    self.mm(self.pb[si][0:nk, rho * nq:(rho + 1) * nq], kwin[bi][:, kcols], q[:, h, qcols], True, True,
                            [("kwin", bi), ("q", h)], [("pb", si)])
                o0 = ndk_off[(d, a)]
                self.stt(tmpS[si][0:nk, 0:n], ndk[0:nk, o0:o0 + n], slope * d / scale, self.pb[si][0:nk, 0:n], ALU.mult, ALU.add,
                         ["ndk", ("pb", si)], [("tmpS", si)])
                self.act(ptS[si][0:nk, 0:n], tmpS[si][0:nk, 0:n], AF.Exp, [("tmpS", si)], [("ptS", si)], scale=scale)
                pend.append((it, si))
                if len(pend) > 2:
                    pv(*pend.pop(0))
                if nxt:
                    nxt.pop(0)()
            while nxt:
                nxt.pop(0)()
            while pend:
                pv(*pend.pop(0))
            self.rcp(rden[:, 0:T], self.pb[5][:, 0:T], [("pb", 5)], ["rden"])
            self.tt(bo[:, h, 0:T], self.pb[4][:, 0:T], rden[:, 0:T], ALU.mult, [("pb", 4), "rden"], [("bo", h)])
            self.act(sq[:, 0:T], bo[:, h, 0:T], AF.Square, [("bo", h)], ["sq"])
            self.mm(self.pb[3][:, 0:T], self.ones[:, :], sq[:, 0:T], h == 0, h == BH - 1, ["ones", "sq"], [("pb", 3)])
        self.act(rstdB[:, 0:T], self.pb[3][:, 0:T], AF.Sqrt, [("pb", 3), "epsc"], ["rstdB"], bias=self.epsc[:, 0:1], scale=1.0 / c.BW)
        self.rcp(rstdB[:, 0:T], rstdB[:, 0:T], ["rstdB"], ["rstdB"])
        for h in range(BH):
            self.stt(hB[:, AH + h, 0:T], bo[:, h, 0:T], self.grpBT[:, h:h + 1], rstdB[:, 0:T], ALU.mult, ALU.mult,
                     [("bo", h), "grpBT", "rstdB"], [hBtok])
        P.barrier()
        self.max_off = max(self.max_off, cv.off)
        cv.off = base
        xres = cv.f32([128, c.TM, D])
        xsb = [cv.bf16([128, D]) for _ in range(2)]
        qx = cv.bf16([128, c.XH, TMX])
        ptX = [cv.bf16([128, TMX]) for _ in range(2)]
        rdx = cv.f32([128, TMX])
        grepO = cv.f32([128, D])
        self.load_grep(grepO, "norm_x_g")
        for tt in range(ntt):
            self.dma("pool", xres[:, tt, :], xrow(tt), [], [("xres", tt)], ("xres", tt))
        self.drain_precast(11)
        wo = self.W["o"]

        def evo(cb, tt, pap, ptok):
            dst = xres[:, tt, cb * wo.cw:(cb + 1) * wo.cw]
            self.tt(dst, pap, dst, ALU.add, [ptok, ("xres", tt)], [("xres", tt)])
        self.linear_tm(wo, hB, hBtok, ntt, evo)
        for tt in range(ntt):
            i = tt % 2
            self.make_h(xres[:, tt, :], ("xres", tt), xsb[i], ("xs", i), grepO, tt * 128, stat_col=4 * i)

        def evxq(g, pap, ptok):
            self.acopy(qx[:, g, 0:T], pap, [ptok], [("qx", g)])
        self.linear_fm(self.W["xq"], hA, hAtok, T, evxq)
        for xh in range(c.XH):
            for mc in range(2):
                self.mm(self.pb[mc][:, 0:T], self.Kx[:, xh, mc * 128:(mc + 1) * 128], qx[:, xh, 0:T], True, True,
                        ["Kx", ("qx", xh)], [("pb", mc)])
                self.act(ptX[mc][:, 0:T], self.pb[mc][:, 0:T], AF.Exp, [("pb", mc)], [("ptX", mc)], scale=scale)
            for mc in range(2):
                self.mm(self.pb[4][:, 0:T], self.Vx[:, mc, xh * 128:(xh + 1) * 128], ptX[mc][:, 0:T], mc == 0, mc == 1,
                        ["Vx", ("ptX", mc)], [("pb", 4)])
                self.mm(self.pb[5][:, 0:T], self.ones[:, :], ptX[mc][:, 0:T], mc == 0, mc == 1, ["ones", ("ptX", mc)], [("pb", 5)])
            self.rcp(rdx[:, 0:T], self.pb[5][:, 0:T], [("pb", 5)], ["rdx"])
            self.tt(hB[:, xh, 0:T], self.pb[4][:, 0:T], rdx[:, 0:T], ALU.mult, [("pb", 4), "rdx"], [hBtok])
        self.load_grep(grepO, "norm_ffn_g")
        wxo = self.W["xo"]

        def evxo(cb, tt, pap, ptok):
            dst = xres[:, tt, cb * wxo.cw:(cb + 1) * wxo.cw]
            self.tt(dst, pap, dst, ALU.add, [ptok, ("xres", tt)], [("xres", tt)])
        self.linear_tm(wxo, hB, hBtok, ntt, evxo)
        for tt in range(ntt):
            self.dma("pool", sg["X2d"][tau0 + tt * 128:tau0 + (tt + 1) * 128, :], xres[:, tt, :], [("xres", tt)], [("X2d", sn)], ("x2d", tt % 2))
            i = tt % 2
            self.make_h(xres[:, tt, :], ("xres", tt), xsb[i], ("xs", i), grepO, tt * 128, stat_col=4 * i)
        self.dma("pool", sg["Hd"].rearrange("(kc p) t -> p kc t", p=128)[:, :, tau0:tau0 + T], hA[:, :, 0:T], [hAtok], [("Hd", sn)], ("hd", 0))
        self.max_off = max(self.max_off, cv.off)

    def attn_chunks(self, sg, tau0, T):
        H, LK = sg["H"], sg["LK"]
        bitems, vload, voff = [], [], {}
        off = 0
        ci = 0
        wlo, whi = 10 ** 9, 0
        for d in DIL:
            m_lo = (tau0 + H) // d
            nqs = T // d
            k_lo = max(0, m_lo - 64)
            k_hi = min(LK // d, m_lo + nqs + 64)
            a = k_lo
            while a < k_hi:
                nk = min(128, k_hi - a)
                vload.append((d, a, nk, ci))
                voff[(d, a)] = off
                off += d * 128
                qa = max(m_lo, a - 64)
                qb = min(m_lo + nqs, a + nk + 64)
                if qb > qa:
                    bitems.append((d, a, nk, qa, qb - qa, ci))
                ci += d
                wlo = min(wlo, d * a)
                whi = max(whi, d * (a + nk))
                a += nk
        assert ci <= 64
        bitems.sort(key=lambda it: -it[0])
        return dict(bitems=bitems, vload=vload, voff=voff, wlo=wlo, wn=whi - wlo, vtot=off, ncol=ci)

    def ffn_pass(self, sg):
        nblk = (sg["LM"] - 2 * sg["MH"]) // 512
        for b in range(nblk):
            self.ffn_block(sg, b, nblk)

    def ffn_block(self, sg, b, nblk):
        c, P = self.c, self.P
        D, KC, GF = c.D, c.KC, c.GF
        MH = sg["MH"]
        sn = sg["name"]
        tau = MH + b * 512
        hA, hAtok = self.hA, self.hAtok
        P.barrier()
        cv = self.carve()
        xres = cv.f32([128, 4, D])
        gch = [cv.bf16([128, c.FFG, 512]) for _ in range(2)]
        tg = [cv.f32([128, 256]) for _ in range(4)]
        tv = [cv.f32([128, 256]) for _ in range(4)]
        fg = cv.f32([128, D])
        ysb = [cv.bf16([128, D])] * 2
        Hv = sg["Hd"].rearrange("(kc p) t -> p kc t", p=128)
        lo, hi = tau - 1, tau + 513
        clo, chi = max(lo, 0), min(hi, sg["LM"])
        self.dma("pool", hA[:, :, clo - lo:chi - lo], Hv[:, :, clo:chi], [("Hd", sn)], [hAtok], ("hd", 1))
        if lo < 0:
            self.mset(hA[:, :, 0:1], 0.0, [hAtok])
        elif b == 0:
            self.ts(hA[:, :, 0:1], hA[:, :, 0:1], self.hv[:, 0:1], None, ALU.mult, None, [hAtok, "hv"], [hAtok])
        if hi > sg["LM"]:
            self.mset(hA[:, :, 513:514], 0.0, [hAtok])
        elif b == nblk - 1:
            self.ts(hA[:, :, 513:514], hA[:, :, 513:514], self.hv[:, 1:2], None, ALU.mult, None, [hAtok, "hv"], [hAtok])
        for tt in range(4):
            self.dma("pool", xres[:, tt, :], sg["X2d"][tau + tt * 128:tau + (tt + 1) * 128, :], [("X2d", sn)], [("xres", tt)], ("xres", tt))
        self.dma("pool", fg, self.final_g.partition_broadcast(128), [], ["fg"], ("cA", "fg"))
        wup, wdn = self.W["up"], self.W["dn"]
        gpp = wup.cw // 128
        halves = ((0, 258), (256, 258))
        f0 = 0
        while f0 < GF:
            nf = min(c.FFG, GF - f0)
            gi = self.rr("gch", 2)
            gbuf, gtok = gch[gi], ("gch", gi)
            grp = f0
            while grp < f0 + nf:
                ng = min(gpp, f0 + nf - grp)
                res = {}
                for which, gbase in (("g", 0), ("v", GF)):
                    cg0 = gbase + grp
                    cbi, gin = divmod(cg0, gpp)
                    assert gin + ng <= gpp
                    for kp in range(wup.nkp):
                        view, tok = wup.load(cbi, kp)
                        for j in range(ng):
                            for hf, (c0, n) in enumerate(halves):
                                bk = j * 2 + hf
                                for kc in range(wup.kcs(kp)):
                                    kk = kp * wup.kcp + kc
                                    self.mm(self.pb[bk][:, 0:n], view[:, kc, (gin + j) * 128:(gin + j + 1) * 128], hA[:, kk, c0:c0 + n],
                                            kk == 0, kk == KC - 1, [tok, hAtok], [("pb", bk)])
                    for j in range(ng):
                        for hf in range(2):
                            bk = j * 2 + hf
                            cg = cg0 + j
                            tb = (tg if which == "g" else tv)[bk]
                            ttok = ("t" + which, bk)
                            pbk = self.pb[bk]
                            self.act(tb[:], pbk[:, 1:257], AF.Identity, [("pb", bk), "convw", "convb"], [ttok],
                                     bias=self.cb[:, cg:cg + 1], scale=self.cw[:, 1, cg:cg + 1])
                            self.stt(tb[:], pbk[:, 0:256], self.cw[:, 0, cg:cg + 1], tb[:], ALU.mult, ALU.add, [("pb", bk), "convw", ttok], [ttok])
                            self.stt(tb[:], pbk[:, 2:258], self.cw[:, 2, cg:cg + 1], tb[:], ALU.mult, ALU.add, [("pb", bk), "convw", ttok], [ttok])
                            res[(which, j, hf)] = (tb, ttok)
                for j in range(ng):
                    for hf in range(2):
                        tbg, tokg = res[("g", j, hf)]
                        tbv, tokv = res[("v", j, hf)]
                        self.act(tbg[:], tbg[:], AF.Silu, [tokg], [tokg])
                        self.tt(gbuf[:, grp - f0 + j, hf * 256:(hf + 1) * 256], tbg[:], tbv[:], ALU.mult, [tokg, tokv], [gtok])
                grp += ng
            kp0, kp1 = f0 // wdn.kcp, -(-(f0 + nf) // wdn.kcp)
            for cbd in range(wdn.ncb):
                views = [wdn.load(cbd, kp) for kp in range(kp0, kp1)]
                for tt in range(4):
                    bk = 4 + self.rr("pbd", 2)
                    for jf in range(nf):
                        view, tok = views[(f0 + jf) // wdn.kcp - kp0]
                        kc = (f0 + jf) % wdn.kcp
                        self.mm(self.pb[bk][:, 0:wdn.cw], gbuf[:, jf, tt * 128:(tt + 1) * 128], view[:, kc, :], jf == 0, jf == nf - 1,
                                [tok, gtok], [("pb", bk)])
                    dst = xres[:, tt, cbd * wdn.cw:(cbd + 1) * wdn.cw]
                    self.tt(dst, self.pb[bk][:, 0:wdn.cw], dst, ALU.add, [("pb", bk), ("xres", tt)], [("xres", tt)])
            f0 += nf
        st = self.stat
        for tt in range(4):
            i = tt % 2
            xr = xres[:, tt, :]
            s = 40 + 4 * i
            self.act(ysb[i][:], xr, AF.Square, [("xres", tt)], ["ysb", ("fst", i)], accum_out=st[:, s:s + 1])
            self.act(st[:, s + 1:s + 2], st[:, s:s + 1], AF.Sqrt, [("fst", i), "epsc"], [("fst1", i)], bias=self.epsc[:, 0:1], scale=1.0 / D)
            self.rcp(st[:, s + 2:s + 3], st[:, s + 1:s + 2], [("fst1", i)], [("fst2", i)])
            self.stt(xr, xr, st[:, s + 2:s + 3], fg[:], ALU.mult, ALU.mult, [("xres", tt), ("fst2", i), "fg"], [("xres", tt)])
            self.dma("pool", sg["y"][b * 512 + tt * 128:b * 512 + (tt + 1) * 128, :], xr, [("xres", tt)], [], ("yout", tt % 2))
        self.max_off = max(self.max_off, cv.off)


def _nd_matrix():
    i = np.arange(128)[:, None]
    J = np.arange(256)[None, :]
    dist = np.abs(J - 64 - i)
    return np.where(dist <= 64, -dist, -NEGBIG).astype(np.float32)


def run(cfg, inputs):
    c = cfg
    x_prompt = np.asarray(inputs["x_prompt"], np.float32)
    x_sample = np.asarray(inputs["x_sample"], np.float32)
    D = c.D
    nc = Kern(cfg).build()
    f = lambda n: np.ascontiguousarray(np.asarray(inputs[n], np.float32)[0])
    pk = lambda v: np.ascontiguousarray(v.reshape(-1, 128).T)
    shared = dict(
        w_in=f("w_in"), w_out=f("w_out"), w_xq=f("w_xq"), w_xkv=f("w_xkv"), w_xo=f("w_xo"), w_up=f("w_up"), w_down=f("w_down"),
        norm_mix_g=pk(f("norm_mix_g")), norm_x_g=pk(f("norm_x_g")), mem_norm_g=pk(f("mem_norm_g")), norm_ffn_g=pk(f("norm_ffn_g")),
        final_g=np.ascontiguousarray(np.asarray(inputs["final_g"], np.float32)),
        norm_mix_g_v=f("norm_mix_g"), norm_x_g_v=f("norm_x_g"), mem_norm_g_v=f("mem_norm_g"), norm_ffn_g_v=f("norm_ffn_g"),
        sg_ln_g=f("sg_ln_g"), sg_ln_b=f("sg_ln_b"), grp_a_g=f("grp_a_g"), grp_b_g=pk(f("grp_b_g")),
        sg_wT=np.ascontiguousarray(np.transpose(f("sg_w"), (2, 0, 1))),
        sg_bT=np.ascontiguousarray(f("sg_b").T),
        conv_w=np.ascontiguousarray(np.transpose(f("conv_w").reshape(3, -1, 128), (2, 0, 1))), conv_b=pk(f("conv_b")),
        ndm=_nd_matrix(), identm=np.eye(128, dtype=np.float32),
    )
    halo = c.MH + c.KH
    in_maps = []
    for core in range(8):
        t0 = core * c.OWN
        lo, hi = t0 - halo, t0 + c.OWN + halo
        xp = np.zeros((c.LK_P, D), np.float32)
        kb = np.full((c.LK_P, 1), -NEGBIG, np.float32)
        a, b = max(lo, 0), min(hi, c.SEQ_P)
        xp[a - lo:b - lo] = x_prompt[0, a:b]
        kb[a - lo:b - lo] = 0.0
        hval = np.zeros((128, 2), np.float32)
        hval[:, 0] = 1.0 if t0 - 1 >= 0 else 0.0
        hval[:, 1] = 1.0 if t0 + c.OWN < c.SEQ_P else 0.0
        m = dict(shared)
        m.update(xp=xp, kbp=kb, xs=np.ascontiguousarray(x_sample[core]), kbs=np.zeros((c.LK_S, 1), np.float32), hval=hval,
                 memp=np.ascontiguousarray(np.asarray(inputs["mem_prompt"], np.float32)[0]),
                 mems=np.ascontiguousarray(np.asarray(inputs["mem_sample"], np.float32)[core]))
        in_maps.append(m)
    res = run_bass_kernel_spmd(nc, in_maps, core_ids=list(range(8)))
    y_prompt = np.concatenate([res.results[i]["yp"] for i in range(8)], axis=0)[None]
    y_sample = np.stack([res.results[i]["ys"] for i in range(8)], axis=0)
    return (y_prompt.astype(np.float32), y_sample.astype(np.float32))


def kernel(**inputs):
    return run(Cfg(), inputs)
```

```python
import contextlib
import math
import numpy as np
import concourse.bass as bass
import concourse.mybir as mybir
from concourse.bass_utils import run_bass_kernel_spmd

F32 = mybir.dt.float32
BF16 = mybir.dt.bfloat16
AF = mybir.ActivationFunctionType
ALU = mybir.AluOpType

SAME_ENGINE_SYNC = True
EPS = 1e-6
NEGBIG = 30000.0
DEN_MERGE = True


class _Op:
    __slots__ = ("eng", "fn", "deps", "is_dma", "dma_key", "sig", "val", "idx")

    def __init__(self, eng, fn, is_dma, dma_key):
        self.eng = eng
        self.fn = fn
        self.deps = []
        self.is_dma = is_dma
        self.dma_key = dma_key
        self.sig = False
        self.val = 0
        self.idx = 0


class Prog:
    ENGS = ("pe", "act", "dve", "pool", "sp")

    def __init__(self, nc):
        self.nc = nc
        self.ops = {e: [] for e in self.ENGS}
        self.last_w = {}
        self.readers = {}
        self.n_ops = 0
        self.last_eng = {}
        self.last_dma = {}

    def op(self, eng, fn, reads=(), writes=(), dma=None, extra_deps=()):
        o = _Op(eng, fn, dma is not None, dma)
        o.idx = self.n_ops
        self.n_ops += 1
        deps = {}
        for t in reads:
            w = self.last_w.get(t)
            if w is not None:
                deps[id(w)] = w
        for t in writes:
            w = self.last_w.get(t)
            if w is not None:
                deps[id(w)] = w
            rs = self.readers.get(t)
            if rs:
                for r in rs.values():
                    deps[id(r)] = r
        for d in extra_deps:
            deps[id(d)] = d
        o.deps = list(deps.values())
        for t in reads:
            rs = self.readers.setdefault(t, {})
            k = ("dma", o.dma_key) if o.is_dma else o.eng
            rs[k] = o
        for t in writes:
            self.last_w[t] = o
            self.readers[t] = {}
        self.ops[eng].append(o)
        if o.is_dma:
            self.last_dma[o.dma_key] = o
        else:
            self.last_eng[eng] = o
        return o

    def barrier(self):
        deps = list(self.last_eng.values()) + list(self.last_dma.values())
        for e in self.ENGS:
            self.op(e, lambda eng: eng.nop(), extra_deps=[d for d in deps])

    def emit(self, stack):
        nc = self.nc
        for e in self.ENGS:
            for o in self.ops[e]:
                for d in o.deps:
                    if d.is_dma:
                        d.sig = True
                    elif d.eng == o.eng and not o.is_dma:
                        if o.eng != "pe" and SAME_ENGINE_SYNC:
                            d.sig = True
                    else:
                        d.sig = True
        sems = {}
        cnt = {}
        for e in self.ENGS:
            for o in self.ops[e]:
                if o.is_dma:
                    k = ("dma", o.dma_key)
                    cnt[k] = cnt.get(k, 0) + 16
                    o.val = cnt[k]
                    o.sig = True
                elif o.sig:
                    k = ("eng", e)
                    cnt[k] = cnt.get(k, 0) + 1
                    o.val = cnt[k]
        final_dma = {k: v for k, v in cnt.items() if k[0] == "dma"}
        for i, k in enumerate(cnt):
            sems[k] = stack.enter_context(nc.semaphore("s%d" % i))
        engmap = {"pe": "tensor", "act": "scalar", "dve": "vector", "pool": "gpsimd", "sp": "sync"}
        block = stack.enter_context(nc.Block())
        prog = self

        def make(e):
            def body(eng):
                known = {}
                for o in prog.ops[e]:
                    need = {}
                    for d in o.deps:
                        if d.is_dma:
                            k = ("dma", d.dma_key)
                        else:
                            if d.eng == e and not o.is_dma:
                                if e == "pe" or not SAME_ENGINE_SYNC:
                                    continue
                            k = ("eng", d.eng)
                        if d.val > need.get(k, 0):
                            need[k] = d.val
                    for k, v in need.items():
                        if known.get(k, 0) >= v:
                            continue
                        known[k] = v
                        eng.wait_ge(sems[k], v)
                    ins = o.fn(eng)
                    if o.is_dma:
                        ins.then_inc(sems[("dma", o.dma_key)], 16)
                    elif o.sig:
                        ins.then_inc(sems[("eng", e)], 1)
                if e == "sp":
                    for k, v in final_dma.items():
                        if known.get(k, 0) < v:
                            eng.wait_ge(sems[k], v)
            return body

        for e in self.ENGS:
            getattr(block, engmap[e])(make(e))


class Cfg:
    def __init__(self, D=4096, AH=16, BH=16, XH=4, DFF=11008, FFG=8, TM=3, ARENA=28880):
        self.D, self.AH, self.BH, self.XH, self.DFF, self.FFG, self.TM = D, AH, BH, XH, DFF, FFG, TM
        self.KC = D // 128
        self.AW, self.BW, self.XW = AH * 128, BH * 128, XH * 128
        self.INW = 2 * self.AW + 3 * self.BW
        self.GF = DFF // 128
        self.NMEM = 256
        self.SEQ_P = 16384
        self.SEQ_S = 2048
        self.OWN = 2048
        self.MH = 128
        self.KH = 1024
        self.LM_P = self.OWN + 2 * self.MH
        self.LK_P = self.LM_P + 2 * self.KH
        self.LM_S = self.SEQ_S
        self.LK_S = self.SEQ_S
        self.ARENA = ARENA


DIL = (1, 4, 16)


class WT:
    def __init__(self, K, name, src, c0, c1, mode):
        self.K = K
        self.name = name
        self.KC = src.shape[0] // 128
        W = c1 - c0
        self.cw = min(256 if mode == "s" else 512, W)
        self.kcp = min(self.KC, 4096 // self.cw)
        self.ncb = W // self.cw
        self.nkp = -(-self.KC // self.kcp)
        self.c0 = c0
        self.dram = K.nc.dram_tensor("wd_" + name, [self.ncb, self.nkp, 128, self.kcp, self.cw], BF16).ap()
        self.srcv = src.rearrange("(kc p) n -> p kc n", p=128)

    def kcs(self, kp):
        return min(self.kcp, self.KC - kp * self.kcp)

    def precast_ops(self):
        return [(self, cb, kp) for cb in range(self.ncb) for kp in range(self.nkp)]

    def precast(self, cb, kp):
        K = self.K
        n = self.kcs(kp)
        dst = self.dram[cb, kp][:, 0:n, :]
        s = self.srcv[:, kp * self.kcp:kp * self.kcp + n, self.c0 + cb * self.cw: self.c0 + (cb + 1) * self.cw]
        key = K.next_cast_key()
        K.dma("pool", dst, s, [], [("wd", self.name, cb, kp), ("castkey", key)], key)

    def load(self, cb, kp):
        K = self.K
        i = K.wslot_next
        K.wslot_next = (i + 1) % K.NSLOT
        n = self.kcs(kp)
        view = K.wslots[i][:, 0:n * self.cw].rearrange("p (a b) -> p a b", b=self.cw)
        K.dma("sp", view, self.dram[cb, kp][:, 0:n, :], [("wd", self.name, cb, kp)], [("ws", i)], ("ws", i))
        return view, ("ws", i)


class Carve:
    def __init__(self, K):
        self.K = K
        self.off = 0

    def f32(self, shape):
        n = int(np.prod(shape[1:]))
        v = self.K.arena[:, self.off:self.off + n]
        self.off += n
        assert self.off <= self.K.arena_words, (self.off, self.K.arena_words)
        return self._shape(v, shape)

    def bf16(self, shape):
        n = int(np.prod(shape[1:]))
        w = (n + 1) // 2
        v = self.K.arena[:, self.off:self.off + w].bitcast(BF16)[:, 0:n]
        self.off += w
        assert self.off <= self.K.arena_words, (self.off, self.K.arena_words)
        return self._shape(v, shape)

    @staticmethod
    def _shape(v, shape):
        if len(shape) == 2:
            return v
        return v.rearrange("p (a b) -> p a b", b=shape[2])


class Kern:
    NSLOT = 6

    def __init__(self, cfg, arena_words=None):
        self.c = cfg
        self.nc = bass.Bass("TRN2", target_bir_lowering=False)
        self.P = Prog(self.nc)
        self.wslot_next = 0
        self.cast_i = 0
        self.rot = {}
        self.arena_words = arena_words or cfg.ARENA
        self.max_off = 0

    def next_cast_key(self):
        self.cast_i += 1
        return ("cast", self.cast_i % 8)

    def rr(self, name, n):
        i = self.rot.get(name, 0)
        self.rot[name] = (i + 1) % n
        return i

    def dma(self, q, out, in_, r, w, key):
        return self.P.op(q, lambda e: e.dma_start(out=out, in_=in_), reads=r, writes=w, dma=key)

    def mm(self, out, lhsT, rhs, start, stop, r, w, **kw):
        return self.P.op("pe", lambda e: e.matmul(out, lhsT=lhsT, rhs=rhs, start=start, stop=stop, **kw), reads=r, writes=w)

    def tr(self, out, in_, r, w):
        ident = self.ident[:]
        return self.P.op("pe", lambda e: e.transpose(out, in_, ident), reads=list(r) + ["ident"], writes=w)

    def act(self, out, in_, func, r, w, **kw):
        return self.P.op("act", lambda e: e.activation(out=out, in_=in_, func=func, **kw), reads=r, writes=w)

    def acopy(self, out, in_, r, w):
        return self.P.op("act", lambda e: e.copy(out=out, in_=in_), reads=r, writes=w)

    def vcopy(self, out, in_, r, w, eng="dve"):
        return self.P.op(eng, lambda e: e.tensor_copy(out=out, in_=in_), reads=r, writes=w)

    def tt(self, out, in0, in1, op, r, w, eng="dve"):
        return self.P.op(eng, lambda e: e.tensor_tensor(out=out, in0=in0, in1=in1, op=op), reads=r, writes=w)

    def ts(self, out, in0, s1, s2, op0, op1, r, w, eng="dve"):
        if s2 is None:
            return self.P.op(eng, lambda e: e.tensor_scalar(out=out, in0=in0, scalar1=s1, scalar2=None, op0=op0), reads=r, writes=w)
        return self.P.op(eng, lambda e: e.tensor_scalar(out=out, in0=in0, scalar1=s1, scalar2=s2, op0=op0, op1=op1), reads=r, writes=w)

    def stt(self, out, in0, scalar, in1, op0, op1, r, w, eng="dve"):
        return self.P.op(eng, lambda e: e.scalar_tensor_tensor(out=out, in0=in0, scalar=scalar, in1=in1, op0=op0, op1=op1), reads=r, writes=w)

    def rcp(self, out, in_, r, w):
        return self.P.op("dve", lambda e: e.reciprocal(out=out, in_=in_), reads=r, writes=w)

    def mset(self, out, val, w, eng="dve"):
        return self.P.op(eng, lambda e: e.memset(out, val), writes=w)

    def build(self):
        c, nc, P = self.c, self.nc, self.P
        D, KC = c.D, c.KC
        inp = lambda n, s, d=F32: nc.dram_tensor(n, list(s), d, kind="ExternalInput").ap()
        self.xp = inp("xp", [c.LK_P, D])
        self.xs = inp("xs", [c.LK_S, D])
        self.kbp = inp("kbp", [c.LK_P, 1])
        self.kbs = inp("kbs", [c.LK_S, 1])
        self.hval = inp("hval", [128, 2])
        self.memp = inp("memp", [c.NMEM, D])
        self.mems = inp("mems", [c.NMEM, D])
        self.w_in = inp("w_in", [D, c.INW])
        self.w_out = inp("w_out", [D, D])
        self.w_xq = inp("w_xq", [D, c.XW])
        self.w_xkv = inp("w_xkv", [D, 2 * c.XW])
        self.w_xo = inp("w_xo", [c.XW, D])
        self.w_up = inp("w_up", [D, 2 * c.DFF])
        self.w_down = inp("w_down", [c.DFF, D])
        self.gvec = {n: inp(n, [128, KC]) for n in ("norm_mix_g", "norm_x_g", "mem_norm_g", "norm_ffn_g")}
        self.final_g = inp("final_g", [D])
        self.gvecD = {n: inp(n + "_v", [D]) for n in ("norm_mix_g", "norm_x_g", "mem_norm_g", "norm_ffn_g")}
        self.sg_ln_g = inp("sg_ln_g", [c.AW])
        self.sg_ln_b = inp("sg_ln_b", [c.AW])
        self.grp_a_g = inp("grp_a_g", [c.AW])
        self.grp_b_g = inp("grp_b_g", [128, c.BH])
        self.sg_wT = inp("sg_wT", [128, c.AH, 128])
        self.sg_bT = inp("sg_bT", [128, c.AH])
        self.conv_w = inp("conv_w", [128, 3, 2 * c.GF])
        self.conv_b = inp("conv_b", [128, 2 * c.GF])
        self.ndm = inp("ndm", [128, 256])
        self.identm = inp("identm", [128, 128])
        self.yp = nc.dram_tensor("yp", [c.OWN, D], F32, kind="ExternalOutput").ap()
        self.ys = nc.dram_tensor("ys", [c.SEQ_S, D], F32, kind="ExternalOutput").ap()
        dt = lambda n, s, d: nc.dram_tensor(n, list(s), d).ap()
        self.seg = {}
        for sname, LK, LM, H, xin, kb, mem, yout, mh in (
            ("p", c.LK_P, c.LM_P, c.KH, self.xp, self.kbp, self.memp, self.yp, c.MH),
            ("s", c.LK_S, c.LM_S, 0, self.xs, self.kbs, self.mems, self.ys, 0),
        ):
            self.seg[sname] = dict(
                name=sname, LK=LK, LM=LM, H=H, x=xin, kb=kb, mem=mem, y=yout, MH=mh,
                Kd=dt("Kd" + sname, [c.BW, LK], BF16), Vd=dt("Vd" + sname, [c.BW, LK], BF16),
                Hd=dt("Hd" + sname, [D, LM], BF16), X2d=dt("X2d" + sname, [LM, D], F32))
        A2 = 2 * c.AW
        self.W = dict(
            k=WT(self, "k", self.w_in, A2 + c.BW, A2 + 2 * c.BW, "s"),
            vB=WT(self, "vB", self.w_in, A2 + 2 * c.BW, A2 + 3 * c.BW, "s"),
            xk=WT(self, "xk", self.w_xkv, 0, c.XW, "s"),
            xv=WT(self, "xv", self.w_xkv, c.XW, 2 * c.XW, "m"),
            u=WT(self, "u", self.w_in, 0, c.AW, "m"),
            vA=WT(self, "vA", self.w_in, c.AW, A2, "m"),
            q=WT(self, "q", self.w_in, A2, A2 + c.BW, "s"),
            o=WT(self, "o", self.w_out, 0, D, "m"),
            xq=WT(self, "xq", self.w_xq, 0, c.XW, "s"),
            xo=WT(self, "xo", self.w_xo, 0, D, "m"),
            up=WT(self, "up", self.w_up, 0, 2 * c.DFF, "s"),
            dn=WT(self, "dn", self.w_down, 0, D, "m"),
        )
        with contextlib.ExitStack() as st:
            self.st = st
            self.alloc()
            self.setup_consts()
            self.precast_q = []
            for n in ("k", "vB", "xk", "xv", "u", "vA", "q", "o", "xq", "xo", "up", "dn"):
                self.precast_q += self.W[n].precast_ops()
            self.drain_precast(len(self.W["k"].precast_ops()) + len(self.W["vB"].precast_ops()))
            for sname in ("p", "s"):
                self.kv_pass(self.seg[sname])
            n_late = len(self.W["up"].precast_ops()) + len(self.W["dn"].precast_ops())
            self.drain_precast(len(self.precast_q) - n_late)
            for sname in ("p", "s"):
                self.mem_kv(self.seg[sname])
                self.mixer_pass(self.seg[sname])
            self.drain_precast(10 ** 9)
            for sname in ("p", "s"):
                self.ffn_pass(self.seg[sname])
            P.emit(st)
        return nc

    def drain_precast(self, n):
        while n > 0 and self.precast_q:
            wt, cb, kp = self.precast_q.pop(0)
            wt.precast(cb, kp)
            n -= 1

    def carve(self):
        if hasattr(self, "_cv"):
            self.max_off = max(self.max_off, self._cv.off)
        self._cv = Carve(self)
        return self._cv

    def alloc(self):
        c, nc, st = self.c, self.nc, self.st
        D, KC = c.D, c.KC
        sb = lambda n, s, d: st.enter_context(nc.sbuf_tensor(n, list(s), d))
        self.wslots = [sb("ws%d" % i, [128, 4096], BF16) for i in range(self.NSLOT)]
        self.hA = sb("hA", [128, KC, 516], BF16)
        self.hAtok = "hA"
        self.ident = sb("ident", [128, 128], BF16)
        self.ones = sb("ones", [128, 128], BF16)
        self.zeros = sb("zeros", [128, 512], BF16)
        self.nd = sb("nd", [128, 256], F32)
        self.epsc = sb("epsc", [128, 1], F32)
        self.gT = {n: sb("gT_" + n, [128, KC], F32) for n in self.gvec}
        self.grpBT = sb("grpBT", [128, c.BH], F32)
        self.sgwT = sb("sgwT", [128, c.AH, 128], BF16)
        self.sgbT = sb("sgbT", [128, c.AH], F32)
        self.rsW = sb("rsW", [128, c.AH], F32)
        self.cw = sb("convw", [128, 3, 2 * c.GF], F32)
        self.cb = sb("convb", [128, 2 * c.GF], F32)
        self.hv = sb("hv", [128, 2], F32)
        self.stat = sb("stat", [128, 64 + 32 * 4], F32)
        self.Kx = sb("Kx", [128, c.XH, c.NMEM], BF16)
        self.Vx = sb("Vx", [128, 2, c.XW], BF16)
        self.arena = sb("arena", [128, self.arena_words], F32)
        ps = lambda n, s, d: st.enter_context(nc.psum_tensor(n, list(s), d))
        self.pb = [ps("pb%d" % i, [128, 512], F32) for i in range(6)]
        self.pt = [ps("pt%d" % i, [128, 1024], BF16) for i in range(2)]

    def vt_elems(self, T):
        tot = 0
        for d in DIL:
            nk = T // d + 128
            tot += (-(-nk // 128)) * d * 128
        return tot

    def setup_consts(self):
        c = self.c
        self.dma("pool", self.ident[:], self.identm, [], ["ident"], "k_ident")
        self.dma("pool", self.nd[:], self.ndm, [], ["nd"], "k_nd")
        self.dma("pool", self.sgwT[:], self.sg_wT, [], ["sgwT"], "k_sgwT")
        self.dma("pool", self.sgbT[:], self.sg_bT, [], ["sgbT"], "k_sgbT")
        self.dma("pool", self.hv[:], self.hval, [], ["hv"], "k_hv")
        for i, (n, t) in enumerate(self.gT.items()):
            self.dma("pool", t[:], self.gvec[n], [], ["gT_" + n], "k_gT_" + n)
        self.dma("pool", self.grpBT[:], self.grp_b_g, [], ["grpBT"], "k_grpBT")
        self.dma("pool", self.cw[:], self.conv_w, [], ["convw"], "k_convw")
        self.dma("pool", self.cb[:], self.conv_b, [], ["convb"], "k_convb")
        self.mset(self.ones[:], 1.0, ["ones"])
        self.mset(self.zeros[:], 0.0, ["zeros"])
        self.mset(self.epsc[:], EPS, ["epsc"])
        for g in range(c.AH):
            self.mm(self.pb[4][:, g:g + 1], self.sgwT[:, g, :], self.ones[:, 0:1], True, True, ["sgwT", "ones"], [("pb", 4)])
        self.vcopy(self.rsW[:], self.pb[4][:, 0:c.AH], [("pb", 4)], ["rsW"])

    def load_grep(self, buf, gname):
        self.dma("pool", buf, self.gvecD[gname].partition_broadcast(128), [], ["grep"], ("cA", "grep"))

    def make_h(self, src_tile, src_tok, xs_buf, xs_tok, grep, col0, stat_col=0, h=None, htok=None):
        c = self.c
        D, KC = c.D, c.KC
        if h is None:
            h, htok = self.hA, self.hAtok
        ssq = self.stat[:, stat_col:stat_col + 1]
        rt = self.stat[:, stat_col + 1:stat_col + 2]
        rstd = self.stat[:, stat_col + 2:stat_col + 3]
        s0, s1, s2 = ("st", stat_col), ("st", stat_col + 1), ("st", stat_col + 2)
        self.act(xs_buf, src_tile, AF.Square, [src_tok], [xs_tok, s0], accum_out=ssq)
        self.act(rt, ssq, AF.Sqrt, [s0, "epsc"], [s1], bias=self.epsc[:, 0:1], scale=1.0 / D)
        self.rcp(rstd, rt, [s1], [s2])
        self.stt(xs_buf, src_tile, rstd, grep, ALU.mult, ALU.mult, [src_tok, s2, "grep"], [xs_tok])
        G = min(8, KC)
        for kg in range(KC // G):
            b = self.rr("pt", 2)
            for j in range(G):
                kc = kg * G + j
                self.tr(self.pt[b][:, j * 128:(j + 1) * 128], xs_buf[:, kc * 128:(kc + 1) * 128], [xs_tok], [("pt", b)])
            src = self.pt[b][:, 0:G * 128].rearrange("p (a b) -> p a b", b=128)
            dst = h[:, kg * G:(kg + 1) * G, col0:col0 + 128]
            self.vcopy(dst, src, [("pt", b)], [htok])

    def linear_fm(self, wt, h, htok, T, evac, tcol0=0):
        ngrp = wt.cw // 128
        for cb in range(wt.ncb):
            banks = [self.rr("pbl", 4) for _ in range(ngrp)]
            for kp in range(wt.nkp):
                view, tok = wt.load(cb, kp)
                for g in range(ngrp):
                    for kc in range(wt.kcs(kp)):
                        kk = kp * wt.kcp + kc
                        self.mm(self.pb[banks[g]][:, 0:T], view[:, kc, g * 128:(g + 1) * 128], h[:, kk, tcol0:tcol0 + T],
                                kk == 0, kk == wt.KC - 1, [tok, htok], [("pb", banks[g])])
            for g in range(ngrp):
                evac(cb * ngrp + g, self.pb[banks[g]][:, 0:T], ("pb", banks[g]))

    def linear_tm(self, wt, h, htok, ntt, evac):
        for cb in range(wt.ncb):
            for kp in range(wt.nkp):
                view, tok = wt.load(cb, kp)
                for tt in range(ntt):
                    for kc in range(wt.kcs(kp)):
                        kk = kp * wt.kcp + kc
                        self.mm(self.pb[tt][:, 0:wt.cw], h[:, kk, tt * 128:(tt + 1) * 128], view[:, kc, :],
                                kk == 0, kk == wt.KC - 1, [tok, htok], [("pb", tt)])
                    if kp == wt.nkp - 1:
                        evac(cb, tt, self.pb[tt][:, 0:wt.cw], ("pb", tt))

    def kv_pass(self, sg):
        c, P = self.c, self.P
        D = c.D
        P.barrier()
        cv = self.carve()
        xt = [cv.f32([128, D]) for _ in range(4)]
        xsb = [cv.bf16([128, D]) for _ in range(2)]
        kst = [cv.bf16([128, 512]) for _ in range(4)]
        vst = [cv.bf16([128, 512]) for _ in range(4)]
        grep = cv.f32([128, D])
        self.load_grep(grep, "norm_mix_g")
        ntiles = sg["LK"] // 128
        hA, htok = self.hA, self.hAtok
        wv = self.W["vB"]
        sn = sg["name"]
        blocks = []
        t0 = 0
        while t0 < ntiles:
            ntt = min(4, ntiles - t0)
            blocks.append((t0, ntt))
            t0 += ntt

        def issue_loads(blk):
            t0, ntt = blk
            for tt in range(ntt):
                k = t0 * 128 + tt * 128
                self.dma("pool", xt[tt], sg["x"][k:k + 128, :], [], [("xt", tt)], ("xt", tt))
        issue_loads(blocks[0])
        for bi, (t0, ntt) in enumerate(blocks):
            T = ntt * 128
            k0 = t0 * 128
            for tt in range(ntt):
                i = tt % 2
                self.make_h(xt[tt], ("xt", tt), xsb[i], ("xs", i), grep, tt * 128, stat_col=4 * i)
            if bi + 1 < len(blocks):
                issue_loads(blocks[bi + 1])
            self.drain_precast(8)

            def evk(g, pap, ptok, T=T, k0=k0):
                i = self.rr("kst", 4)
                self.acopy(kst[i][:, 0:T], pap, [ptok], [("kst", i)])
                self.dma("pool", sg["Kd"][g * 128:(g + 1) * 128, k0:k0 + T], kst[i][:, 0:T], [("kst", i)], [("Kd", sn)], ("kst", i))
            self.linear_fm(self.W["k"], hA, htok, T, evk)

            def evv(g, pap, ptok, T=T, k0=k0):
                i = self.rr("vst", 4)
                self.vcopy(vst[i][:, 0:T], pap, [ptok], [("vst", i)])
                self.dma("pool", sg["Vd"][g * 128:(g + 1) * 128, k0:k0 + T], vst[i][:, 0:T], [("vst", i)], [("Vd", sn)], ("vst", i))
            self.linear_fm(wv, hA, htok, T, evv)

    def mem_kv(self, sg):
        c, P = self.c, self.P
        D = c.D
        P.barrier()
        cv = self.carve()
        xt = [cv.f32([128, D]) for _ in range(2)]
        xsb = [cv.bf16([128, D]) for _ in range(2)]
        hA, htok = self.hA, self.hAtok
        grep = cv.f32([128, D])
        self.load_grep(grep, "mem_norm_g")
        for tt in range(2):
            self.dma("pool", xt[tt], sg["mem"][tt * 128:(tt + 1) * 128, :], [], [("xt", tt)], ("xt", tt))
            self.make_h(xt[tt], ("xt", tt), xsb[tt], ("xs", tt), grep, tt * 128, stat_col=4 * tt)

        def evk(g, pap, ptok):
            self.acopy(self.Kx[:, g, :], pap, [ptok], ["Kx"])
        self.linear_fm(self.W["xk"], hA, htok, c.NMEM, evk)
        wv = self.W["xv"]

        def evv(cb, tt, pap, ptok):
            self.acopy(self.Vx[:, tt, cb * wv.cw:(cb + 1) * wv.cw], pap, [ptok], ["Vx"])
        self.linear_tm(wv, hA, htok, 2, evv)

    def mixer_pass(self, sg):
        c = self.c
        ntiles = sg["LM"] // 128
        t0 = 0
        while t0 < ntiles:
            ntt = min(c.TM, ntiles - t0)
            self.mixer_block(sg, t0 * 128, ntt)
            t0 += ntt

    def mixer_block(self, sg, tau0, ntt):
        c, P = self.c, self.P
        D, KC = c.D, c.KC
        T = ntt * 128
        TMX = c.TM * 128
        H = sg["H"]
        sn = sg["name"]
        hA, hAtok = self.hA, self.hAtok
        scale = 128 ** -0.5
        st = self.stat
        P.barrier()
        cv = self.carve()
        hB = cv.bf16([128, KC, TMX])
        hBtok = "hB"
        base = cv.off
        xt = [cv.f32([128, D]) for _ in range(2)]
        xsb = [cv.bf16([128, D]) for _ in range(2)]
        xrow = lambda tt: sg["x"][H + tau0 + tt * 128: H + tau0 + (tt + 1) * 128, :]
        grepN = cv.f32([128, D])
        self.load_grep(grepN, "norm_mix_g")
        for tt in range(ntt):
            i = tt % 2
            self.dma("pool", xt[i], xrow(tt), [], [("xt", i)], ("xt", i))
            self.make_h(xt[i], ("xt", i), xsb[i], ("xs", i), grepN, tt * 128, stat_col=4 * i)
        P.barrier()
        self.max_off = max(self.max_off, cv.off)
        cv.off = base
        AW, AH = c.AW, c.AH
        u = cv.bf16([128, c.TM, AW])
        vA = cv.bf16([128, c.TM, AW])
        lnG = cv.f32([128, AW])
        lnB = cv.f32([128, AW])
        grA = cv.f32([128, AW])
        Cb = cv.f32([128, AW])
        tmpA = [cv.f32([128, 512]) for _ in range(2)]
        aouts = [lnB] + [cv.f32([128, AW]) for _ in range(c.TM - 1)]
        ans = [cv.bf16([128, AW]) for _ in range(c.TM)]
        for dst, src, nm in ((lnG, self.sg_ln_g, "lnG"), (lnB, self.sg_ln_b, "lnB"), (grA, self.grp_a_g, "grA")):
            self.dma("pool", dst, src.partition_broadcast(128), [], [nm], ("cA", nm))
        for g in range(AH):
            self.ts(Cb[:, g * 128:(g + 1) * 128], lnB[:, g * 128:(g + 1) * 128], self.rsW[:, g:g + 1], self.sgbT[:, g:g + 1],
                    ALU.mult, ALU.add, ["lnB", "rsW", "sgbT"], ["Cb"])
        self.drain_precast(11)
        wu, wva = self.W["u"], self.W["vA"]

        def evu(cb, tt, pap, ptok):
            self.act(u[:, tt, cb * wu.cw:(cb + 1) * wu.cw], pap, AF.Gelu, [ptok], [("u", tt)])
        self.linear_tm(wu, hA, hAtok, ntt, evu)

        def evva(cb, tt, pap, ptok):
            self.act(vA[:, tt, cb * wva.cw:(cb + 1) * wva.cw], pap, AF.Gelu, [ptok], [("vA", tt)])
        self.linear_tm(wva, hA, hAtok, ntt, evva)
        bw = min(512, max(64, AW // 2))
        nbn = AW // bw
        S0 = 64
        sc = lambda tt, j: S0 + 32 * tt + j
        for tt in range(ntt):
            for j in range(nbn):
                P.op("dve", (lambda o, i: (lambda e: e.bn_stats(out=o, in_=i)))(st[:, sc(tt, 8 + 6 * j):sc(tt, 14 + 6 * j)], vA[:, tt, j * bw:(j + 1) * bw]),
                     reads=[("vA", tt)], writes=[("bn", tt, j)])
            P.op("dve", (lambda o, i: (lambda e: e.bn_aggr(out=o, in_=i)))(st[:, sc(tt, 0):sc(tt, 2)],
                                                                         st[:, sc(tt, 8):sc(tt, 8 + 6 * nbn)].rearrange("p (a b) -> p a b", b=6)),
                 reads=[("bn", tt, j) for j in range(nbn)], writes=[("mv", tt)])
        for tt in range(ntt):
            self.act(st[:, sc(tt, 2):sc(tt, 3)], st[:, sc(tt, 1):sc(tt, 2)], AF.Sqrt, [("mv", tt), "epsc"], [("lnr", tt)], bias=self.epsc[:, 0:1], scale=1.0)
        for tt in range(ntt):
            self.rcp(st[:, sc(tt, 3):sc(tt, 4)], st[:, sc(tt, 2):sc(tt, 3)], [("lnr", tt)], [("lnr2", tt)])
        for tt in range(ntt):
            self.ts(vA[:, tt, :], vA[:, tt, :], st[:, sc(tt, 0):sc(tt, 1)], st[:, sc(tt, 3):sc(tt, 4)], ALU.subtract, ALU.mult,
                    [("vA", tt), ("mv", tt), ("lnr2", tt)], [("vA", tt)])
        hpb = min(4, AH)
        for tt in range(ntt):
            aout = aouts[tt]
            for g0 in range(0, AH, hpb):
                bk = 4 + self.rr("pbm", 2)
                for g in range(g0, g0 + hpb):
                    self.mm(self.pb[bk][:, (g - g0) * 128:(g - g0 + 1) * 128], self.sgwT[:, g, :], vA[:, tt, g * 128:(g + 1) * 128],
                            True, True, ["sgwT", ("vA", tt)], [("pb", bk)])
                cols = slice(g0 * 128, (g0 + hpb) * 128)
                n = hpb * 128
                ti = self.rr("tmpA", 2)
                tA, ttok = tmpA[ti], ("tmpA", ti)
                self.tt(tA[:, 0:n], self.pb[bk][:, 0:n], lnG[:, cols], ALU.mult, [("pb", bk), "lnG"], [ttok])
                self.tt(tA[:, 0:n], tA[:, 0:n], Cb[:, cols], ALU.add, [ttok, "Cb"], [ttok], eng="pool")
                wtoks = [("aout", tt)] + (["lnB"] if tt == 0 else [])
                self.tt(aout[:, cols], tA[:, 0:n], u[:, tt, cols], ALU.mult, [ttok, ("u", tt)], wtoks)
        for tt in range(ntt):
            self.act(ans[tt][:], aouts[tt][:], AF.Square, [("aout", tt)], [("an", tt), ("ssA", tt)], accum_out=st[:, sc(tt, 4):sc(tt, 5)])
        for tt in range(ntt):
            self.act(st[:, sc(tt, 5):sc(tt, 6)], st[:, sc(tt, 4):sc(tt, 5)], AF.Sqrt, [("ssA", tt), "epsc"], [("ssA2", tt)],
                     bias=self.epsc[:, 0:1], scale=1.0 / AW)
        for tt in range(ntt):
            self.rcp(st[:, sc(tt, 6):sc(tt, 7)], st[:, sc(tt, 5):sc(tt, 6)], [("ssA2", tt)], [("ssA3", tt)])
        for tt in range(ntt):
            self.stt(ans[tt][:], aouts[tt][:], st[:, sc(tt, 6):sc(tt, 7)], grA[:], ALU.mult, ALU.mult,
                     [("aout", tt), ("ssA3", tt), "grA"], [("an", tt)])
        G = min(8, AH)
        for tt in range(ntt):
            for kg in range(AH // G):
                b = self.rr("pt", 2)
                for j in range(G):
                    g = kg * G + j
                    self.tr(self.pt[b][:, j * 128:(j + 1) * 128], ans[tt][:, g * 128:(g + 1) * 128], [("an", tt)], [("pt", b)])
                self.acopy(hB[:, kg * G:(kg + 1) * G, tt * 128:(tt + 1) * 128],
                           self.pt[b][:, 0:G * 128].rearrange("p (a b) -> p a b", b=128), [("pt", b)], [hBtok])
        P.barrier()
        self.max_off = max(self.max_off, cv.off)
        cv.off = base
        BH = c.BH
        q = cv.bf16([128, BH, TMX])
        WK = TMX + 2 * c.KH
        kwin = [cv.bf16([128, WK]) for _ in range(2)]
        vwin = [cv.bf16([128, WK]) for _ in range(2)]
        ch = self.attn_chunks(sg, tau0, T)
        nve = self.vt_elems(TMX)
        assert ch["vtot"] <= nve
        vts = [cv.bf16([128, nve]) for _ in range(2)]
        bo = cv.bf16([128, BH, TMX])
        kbt = cv.f32([128, 64])
        ndk_off = {}
        _o = 0
        for it in ch["bitems"]:
            ndk_off[(it[0], it[1])] = _o
            _o += it[0] * it[4]
        ndk = cv.f32([128, max(_o, 1)])
        NS = max(TMX, 256)
        ptS = [cv.bf16([128, NS]) for _ in range(3)]
        tmpS = [cv.f32([128, NS]) for _ in range(3)]
        rden = cv.f32([128, TMX])
        sq = cv.bf16([128, TMX])
        rstdB = cv.f32([128, TMX])

        def evq(g, pap, ptok):
            self.acopy(q[:, g, 0:T], pap, [ptok], [("q", g)])
        self.linear_fm(self.W["q"], hA, hAtok, T, evq)
        assert ch["ncol"] <= 48
        for (d, a, nk, ci) in ch["vload"]:
            src = bass.AP(sg["kb"].tensor, sg["kb"].offset + d * a, [[d, nk], [1, d]])
            self.dma("pool", kbt[0:nk, ci:ci + d], src, [], ["kbt"], ("kbt", 0))
        for (d, a, nk, qa, nq, ci) in ch["bitems"]:
            J0 = qa - a + 64
            o0 = ndk_off[(d, a)]
            if d > 1:
                self.tt(ndk[0:nk, o0:o0 + d * nq].rearrange("p (r q) -> p r q", q=nq),
                        self.nd[0:nk, J0:J0 + nq].unsqueeze(1).to_broadcast([nk, d, nq]),
                        kbt[0:nk, ci:ci + d].unsqueeze(2).to_broadcast([nk, d, nq]), ALU.add, ["nd", "kbt"], ["ndk"])
            else:
                self.ts(ndk[0:nk, o0:o0 + nq], self.nd[0:nk, J0:J0 + nq], kbt[0:nk, ci:ci + 1], None, ALU.add, None, ["nd", "kbt"], ["ndk"])
        wlo, wn = ch["wlo"], ch["wn"]
        def load_head(h):
            bi = h % 2
            self.dma("pool", kwin[bi][:, 0:wn], sg["Kd"][h * 128:(h + 1) * 128, wlo:wlo + wn], [("Kd", sn)], [("kwin", bi)], ("kwin", bi))
            self.dma("pool", vwin[bi][:, 0:wn], sg["Vd"][h * 128:(h + 1) * 128, wlo:wlo + wn], [("Vd", sn)], [("vwin", bi)], ("vwin", bi))

        def tr_groups(h):
            bi = h % 2
            tiles = []
            for (d, a, nk, ci) in ch["vload"]:
                for rho in range(d):
                    tiles.append((d, a, nk, rho, ch["voff"][(d, a)] + rho * 128))
            out = []
            ti = 0
            while ti < len(tiles):
                nk0 = tiles[ti][2]
                grp = [tiles[ti]]
                while len(grp) < 8 and ti + len(grp) < len(tiles) and tiles[ti + len(grp)][2] == nk0:
                    grp.append(tiles[ti + len(grp)])

                def emit(grp=grp, nk0=nk0, bi=bi):
                    b = self.rr("pt", 2)
                    for j, (d, a, nk, rho, off) in enumerate(grp):
                        c0 = rho + d * a - wlo
                        self.tr(self.pt[b][0:nk, j * 128:(j + 1) * 128], vwin[bi][:, c0:c0 + d * (nk - 1) + 1:d], [("vwin", bi)], [("pt", b)])
                    off0 = grp[0][4]
                    self.acopy(vts[bi][0:nk0, off0:off0 + len(grp) * 128], self.pt[b][0:nk0, 0:len(grp) * 128], [("pt", b)], [("vt", bi)])
                out.append(emit)
                ti += len(grp)
            return out

        def tr_head(h):
            for g in tr_groups(h):
                g()
        load_head(0)
        tr_head(0)
        for h in range(BH):
            bi = h % 2
            if h + 1 < BH:
                load_head(h + 1)
            slope = 2.0 ** (-8.0 * (h + 1) / BH)
            pend = []
            zeroed = [False]
            nxt = tr_groups(h + 1) if h + 1 < BH else []

            def zero_init():
                if not zeroed[0]:
                    zeroed[0] = True
                    for bk in (4, 5):
                        self.mm(self.pb[bk][:, 0:T], self.zeros[:, 0:128], self.zeros[:, 0:T], True, False, ["zeros"], [("pb", bk)])

            def pv(it, si, bi=bi):
                (d, a, nk, qa, nq, ci) = it
                zero_init()
                for rho in range(d):
                    off = ch["voff"][(d, a)] + rho * 128
                    qcol0 = rho + d * qa - H - tau0
                    cols = slice(qcol0, qcol0 + d * (nq - 1) + 1, d)
                    pslice = ptS[si][0:nk, rho * nq:(rho + 1) * nq]
                    self.mm(self.pb[4][:, cols], vts[bi][0:nk, off:off + 128], pslice, False, False,
                            [("vt", bi), ("ptS", si)], [("pb", 4)], skip_group_check=True)
                    if not (DEN_MERGE and d > 1):
                        self.mm(self.pb[5][:, cols], self.ones[0:nk, :], pslice, False, False,
                                ["ones", ("ptS", si)], [("pb", 5)], skip_group_check=True)
                if DEN_MERGE and d > 1:
                    base = d * qa - H - tau0
                    self.mm(self.pb[5][:, base:base + d * nq].rearrange("p (j r) -> p r j", r=d), self.ones[0:nk, :],
                            ptS[si][0:nk, 0:d * nq].rearrange("p (r q) -> p r q", q=nq), False, False,
                            ["ones", ("ptS", si)], [("pb", 5)], skip_group_check=True)
            for n_it, it in enumerate(ch["bitems"]):
                (d, a, nk, qa, nq, ci) = it
                si = n_it % 3
                n = d * nq
                assert n <= NS
                for rho in range(d):
                    kcol0 = rho + d * a - wlo
                    kcols = slice(kcol0, kcol0 + d * (nk - 1) + 1, d)
                    qcol0 = rho + d * qa - H - tau0
                    qcols = slice(qcol0, qcol0 + d * (nq - 1) + 1, d)
                    self.mm(self.pb[si][0:nk, rho * nq:(rho + 1) * nq], kwin[bi][:, kcols], q[:, h, qcols], True, True,
                            [("kwin", bi), ("q", h)], [("pb", si)])
                o0 = ndk_off[(d, a)]
                self.stt(tmpS[si][0:nk, 0:n], ndk[0:nk, o0:o0 + n], slope * d / scale, self.pb[si][0:nk, 0:n], ALU.mult, ALU.add,
                         ["ndk", ("pb", si)], [("tmpS", si)])
                self.act(ptS[si][0:nk, 0:n], tmpS[si][0:nk, 0:n], AF.Exp, [("tmpS", si)], [("ptS", si)], scale=scale)
                pend.append((it, si))
                if len(pend) > 2:
                    pv(*pend.pop(0))
                if nxt:
                    nxt.pop(0)()
            while nxt:
                nxt.pop(0)()
            while pend:
                pv(*pend.pop(0))
            self.rcp(rden[:, 0:T], self.pb[5][:, 0:T], [("pb", 5)], ["rden"])
            self.tt(bo[:, h, 0:T], self.pb[4][:, 0:T], rden[:, 0:T], ALU.mult, [("pb", 4), "rden"], [("bo", h)])
            self.act(sq[:, 0:T], bo[:, h, 0:T], AF.Square, [("bo", h)], ["sq"])
            self.mm(self.pb[3][:, 0:T], self.ones[:, :], sq[:, 0:T], h == 0, h == BH - 1, ["ones", "sq"], [("pb", 3)])
        self.act(rstdB[:, 0:T], self.pb[3][:, 0:T], AF.Sqrt, [("pb", 3), "epsc"], ["rstdB"], bias=self.epsc[:, 0:1], scale=1.0 / c.BW)
        self.rcp(rstdB[:, 0:T], rstdB[:, 0:T], ["rstdB"], ["rstdB"])
        for h in range(BH):
            self.stt(hB[:, AH + h, 0:T], bo[:, h, 0:T], self.grpBT[:, h:h + 1], rstdB[:, 0:T], ALU.mult, ALU.mult,
                     [("bo", h), "grpBT", "rstdB"], [hBtok])
        P.barrier()
        self.max_off = max(self.max_off, cv.off)
        cv.off = base
        xres = cv.f32([128, c.TM, D])
        xsb = [cv.bf16([128, D]) for _ in range(2)]
        qx = cv.bf16([128, c.XH, TMX])
        ptX = [cv.bf16([128, TMX]) for _ in range(2)]
        rdx = cv.f32([128, TMX])
        grepO = cv.f32([128, D])
        self.load_grep(grepO, "norm_x_g")
        for tt in range(ntt):
            self.dma("pool", xres[:, tt, :], xrow(tt), [], [("xres", tt)], ("xres", tt))
        self.drain_precast(11)
        wo = self.W["o"]

        def evo(cb, tt, pap, ptok):
            dst = xres[:, tt, cb * wo.cw:(cb + 1) * wo.cw]
            self.tt(dst, pap, dst, ALU.add, [ptok, ("xres", tt)], [("xres", tt)])
        self.linear_tm(wo, hB, hBtok, ntt, evo)
        for tt in range(ntt):
            i = tt % 2
            self.make_h(xres[:, tt, :], ("xres", tt), xsb[i], ("xs", i), grepO, tt * 128, stat_col=4 * i)

        def evxq(g, pap, ptok):
            self.acopy(qx[:, g, 0:T], pap, [ptok], [("qx", g)])
        self.linear_fm(self.W["xq"], hA, hAtok, T, evxq)
        for xh in range(c.XH):
            for mc in range(2):
                self.mm(self.pb[mc][:, 0:T], self.Kx[:, xh, mc * 128:(mc + 1) * 128], qx[:, xh, 0:T], True, True,
                        ["Kx", ("qx", xh)], [("pb", mc)])
                self.act(ptX[mc][:, 0:T], self.pb[mc][:, 0:T], AF.Exp, [("pb", mc)], [("ptX", mc)], scale=scale)
            for mc in range(2):
                self.mm(self.pb[4][:, 0:T], self.Vx[:, mc, xh * 128:(xh + 1) * 128], ptX[mc][:, 0:T], mc == 0, mc == 1,
                        ["Vx", ("ptX", mc)], [("pb", 4)])
                self.mm(self.pb[5][:, 0:T], self.ones[:, :], ptX[mc][:, 0:T], mc == 0, mc == 1, ["ones", ("ptX", mc)], [("pb", 5)])
            self.rcp(rdx[:, 0:T], self.pb[5][:, 0:T], [("pb", 5)], ["rdx"])
            self.tt(hB[:, xh, 0:T], self.pb[4][:, 0:T], rdx[:, 0:T], ALU.mult, [("pb", 4), "rdx"], [hBtok])
        self.load_grep(grepO, "norm_ffn_g")
        wxo = self.W["xo"]

        def evxo(cb, tt, pap, ptok):
            dst = xres[:, tt, cb * wxo.cw:(cb + 1) * wxo.cw]
            self.tt(dst, pap, dst, ALU.add, [ptok, ("xres", tt)], [("xres", tt)])
        self.linear_tm(wxo, hB, hBtok, ntt, evxo)
        for tt in range(ntt):
            self.dma("pool", sg["X2d"][tau0 + tt * 128:tau0 + (tt + 1) * 128, :], xres[:, tt, :], [("xres", tt)], [("X2d", sn)], ("x2d", tt % 2))
            i = tt % 2
            self.make_h(xres[:, tt, :], ("xres", tt), xsb[i], ("xs", i), grepO, tt * 128, stat_col=4 * i)
        self.dma("pool", sg["Hd"].rearrange("(kc p) t -> p kc t", p=128)[:, :, tau0:tau0 + T], hA[:, :, 0:T], [hAtok], [("Hd", sn)], ("hd", 0))
        self.max_off = max(self.max_off, cv.off)

    def attn_chunks(self, sg, tau0, T):
        H, LK = sg["H"], sg["LK"]
        bitems, vload, voff = [], [], {}
        off = 0
        ci = 0
        wlo, whi = 10 ** 9, 0
        for d in DIL:
            m_lo = (tau0 + H) // d
            nqs = T // d
            k_lo = max(0, m_lo - 64)
            k_hi = min(LK // d, m_lo + nqs + 64)
            a = k_lo
            while a < k_hi:
                nk = min(128, k_hi - a)
                vload.append((d, a, nk, ci))
                voff[(d, a)] = off
                off += d * 128
                qa = max(m_lo, a - 64)
                qb = min(m_lo + nqs, a + nk + 64)
                if qb > qa:
                    bitems.append((d, a, nk, qa, qb - qa, ci))
                ci += d
                wlo = min(wlo, d * a)
                whi = max(whi, d * (a + nk))
                a += nk
        assert ci <= 64
        bitems.sort(key=lambda it: -it[0])
        return dict(bitems=bitems, vload=vload, voff=voff, wlo=wlo, wn=whi - wlo, vtot=off, ncol=ci)

    def ffn_pass(self, sg):
        nblk = (sg["LM"] - 2 * sg["MH"]) // 512
        self.P.barrier()
        self.ffn_load_h(sg, 0, nblk)
        for b in range(nblk):
            self.ffn_block(sg, b, nblk)

    def ffn_load_h(self, sg, b, nblk):
        hA, hAtok = self.hA, self.hAtok
        sn = sg["name"]
        tau = sg["MH"] + b * 512
        Hv = sg["Hd"].rearrange("(kc p) t -> p kc t", p=128)
        lo, hi = tau - 1, tau + 513
        clo, chi = max(lo, 0), min(hi, sg["LM"])
        self.dma("pool", hA[:, :, clo - lo:chi - lo], Hv[:, :, clo:chi], [("Hd", sn)], [hAtok], ("hd", 1))
        if lo < 0:
            self.mset(hA[:, :, 0:1], 0.0, [hAtok])
        elif b == 0:
            self.ts(hA[:, :, 0:1], hA[:, :, 0:1], self.hv[:, 0:1], None, ALU.mult, None, [hAtok, "hv"], [hAtok])
        if hi > sg["LM"]:
            self.mset(hA[:, :, 513:514], 0.0, [hAtok])
        elif b == nblk - 1:
            self.ts(hA[:, :, 513:514], hA[:, :, 513:514], self.hv[:, 1:2], None, ALU.mult, None, [hAtok, "hv"], [hAtok])

    def ffn_block(self, sg, b, nblk):
        c, P = self.c, self.P
        D, KC, GF = c.D, c.KC, c.GF
        MH = sg["MH"]
        sn = sg["name"]
        tau = MH + b * 512
        hA, hAtok = self.hA, self.hAtok
        cv = self.carve()
        xres = cv.f32([128, 4, D])
        gch = [cv.bf16([128, c.FFG, 512]) for _ in range(2)]
        tg = [cv.f32([128, 256]) for _ in range(4)]
        tv = [cv.f32([128, 256]) for _ in range(4)]
        fg = cv.f32([128, D])
        ysb = [cv.bf16([128, D])] * 2
        for tt in range(4):
            self.dma("pool", xres[:, tt, :], sg["X2d"][tau + tt * 128:tau + (tt + 1) * 128, :], [("X2d", sn)], [("xres", tt)], ("xres", tt))
        self.dma("pool", fg, self.final_g.partition_broadcast(128), [], ["fg"], ("cA", "fg"))
        wup, wdn = self.W["up"], self.W["dn"]
        gpp = wup.cw // 128
        halves = ((0, 258), (256, 258))
        f0 = 0
        while f0 < GF:
            nf = min(c.FFG, GF - f0)
            gi = self.rr("gch", 2)
            gbuf, gtok = gch[gi], ("gch", gi)
            grp = f0
            while grp < f0 + nf:
                ng = min(gpp, f0 + nf - grp)
                res = {}
                for which, gbase in (("g", 0), ("v", GF)):
                    cg0 = gbase + grp
                    cbi, gin = divmod(cg0, gpp)
                    assert gin + ng <= gpp
                    for kp in range(wup.nkp):
                        view, tok = wup.load(cbi, kp)
                        for j in range(ng):
                            for hf, (c0, n) in enumerate(halves):
                                bk = j * 2 + hf
                                for kc in range(wup.kcs(kp)):
                                    kk = kp * wup.kcp + kc
                                    self.mm(self.pb[bk][:, 0:n], view[:, kc, (gin + j) * 128:(gin + j + 1) * 128], hA[:, kk, c0:c0 + n],
                                            kk == 0, kk == KC - 1, [tok, hAtok], [("pb", bk)])
                    for j in range(ng):
                        for hf in range(2):
                            bk = j * 2 + hf
                            cg = cg0 + j
                            tb = (tg if which == "g" else tv)[bk]
                            ttok = ("t" + which, bk)
                            pbk = self.pb[bk]
                            self.act(tb[:], pbk[:, 1:257], AF.Identity, [("pb", bk), "convw", "convb"], [ttok],
                                     bias=self.cb[:, cg:cg + 1], scale=self.cw[:, 1, cg:cg + 1])
                            self.stt(tb[:], pbk[:, 0:256], self.cw[:, 0, cg:cg + 1], tb[:], ALU.mult, ALU.add, [("pb", bk), "convw", ttok], [ttok])
                            self.stt(tb[:], pbk[:, 2:258], self.cw[:, 2, cg:cg + 1], tb[:], ALU.mult, ALU.add, [("pb", bk), "convw", ttok], [ttok])
                            res[(which, j, hf)] = (tb, ttok)
                for j in range(ng):
                    for hf in range(2):
                        tbg, tokg = res[("g", j, hf)]
                        tbv, tokv = res[("v", j, hf)]
                        self.act(tbg[:], tbg[:], AF.Silu, [tokg], [tokg])
                        self.tt(gbuf[:, grp - f0 + j, hf * 256:(hf + 1) * 256], tbg[:], tbv[:], ALU.mult, [tokg, tokv], [gtok])
                grp += ng
            if f0 + nf >= GF and b + 1 < nblk:
                self.ffn_load_h(sg, b + 1, nblk)
            kp0, kp1 = f0 // wdn.kcp, -(-(f0 + nf) // wdn.kcp)
            for cbd in range(wdn.ncb):
                views = [wdn.load(cbd, kp) for kp in range(kp0, kp1)]
                for tt in range(4):
                    bk = 4 + self.rr("pbd", 2)
                    for jf in range(nf):
                        view, tok = views[(f0 + jf) // wdn.kcp - kp0]
                        kc = (f0 + jf) % wdn.kcp
                        self.mm(self.pb[bk][:, 0:wdn.cw], gbuf[:, jf, tt * 128:(tt + 1) * 128], view[:, kc, :], jf == 0, jf == nf - 1,
                                [tok, gtok], [("pb", bk)])
                    dst = xres[:, tt, cbd * wdn.cw:(cbd + 1) * wdn.cw]
                    self.tt(dst, self.pb[bk][:, 0:wdn.cw], dst, ALU.add, [("pb", bk), ("xres", tt)], [("xres", tt)])
            f0 += nf
        st = self.stat
        for tt in range(4):
            i = tt % 2
            xr = xres[:, tt, :]
            s = 40 + 4 * i
            self.act(ysb[i][:], xr, AF.Square, [("xres", tt)], ["ysb", ("fst", i)], accum_out=st[:, s:s + 1])
            self.act(st[:, s + 1:s + 2], st[:, s:s + 1], AF.Sqrt, [("fst", i), "epsc"], [("fst1", i)], bias=self.epsc[:, 0:1], scale=1.0 / D)
            self.rcp(st[:, s + 2:s + 3], st[:, s + 1:s + 2], [("fst1", i)], [("fst2", i)])
            self.stt(xr, xr, st[:, s + 2:s + 3], fg[:], ALU.mult, ALU.mult, [("xres", tt), ("fst2", i), "fg"], [("xres", tt)])
            self.dma("pool", sg["y"][b * 512 + tt * 128:b * 512 + (tt + 1) * 128, :], xr, [("xres", tt)], [], ("yout", tt))
        self.max_off = max(self.max_off, cv.off)


def _nd_matrix():
    i = np.arange(128)[:, None]
    J = np.arange(256)[None, :]
    dist = np.abs(J - 64 - i)
    return np.where(dist <= 64, -dist, -NEGBIG).astype(np.float32)


def run(cfg, inputs):
    c = cfg
    x_prompt = np.asarray(inputs["x_prompt"], np.float32)
    x_sample = np.asarray(inputs["x_sample"], np.float32)
    D = c.D
    nc = Kern(cfg).build()
    f = lambda n: np.ascontiguousarray(np.asarray(inputs[n], np.float32)[0])
    pk = lambda v: np.ascontiguousarray(v.reshape(-1, 128).T)
    shared = dict(
        w_in=f("w_in"), w_out=f("w_out"), w_xq=f("w_xq"), w_xkv=f("w_xkv"), w_xo=f("w_xo"), w_up=f("w_up"), w_down=f("w_down"),
        norm_mix_g=pk(f("norm_mix_g")), norm_x_g=pk(f("norm_x_g")), mem_norm_g=pk(f("mem_norm_g")), norm_ffn_g=pk(f("norm_ffn_g")),
        final_g=np.ascontiguousarray(np.asarray(inputs["final_g"], np.float32)),
        norm_mix_g_v=f("norm_mix_g"), norm_x_g_v=f("norm_x_g"), mem_norm_g_v=f("mem_norm_g"), norm_ffn_g_v=f("norm_ffn_g"),
        sg_ln_g=f("sg_ln_g"), sg_ln_b=f("sg_ln_b"), grp_a_g=f("grp_a_g"), grp_b_g=pk(f("grp_b_g")),
        sg_wT=np.ascontiguousarray(np.transpose(f("sg_w"), (2, 0, 1))),
        sg_bT=np.ascontiguousarray(f("sg_b").T),
        conv_w=np.ascontiguousarray(np.transpose(f("conv_w").reshape(3, -1, 128), (2, 0, 1))), conv_b=pk(f("conv_b")),
        ndm=_nd_matrix(), identm=np.eye(128, dtype=np.float32),
    )
    halo = c.MH + c.KH
    in_maps = []
    for core in range(8):
        t0 = core * c.OWN
        lo, hi = t0 - halo, t0 + c.OWN + halo
        xp = np.zeros((c.LK_P, D), np.float32)
        kb = np.full((c.LK_P, 1), -NEGBIG, np.float32)
        a, b = max(lo, 0), min(hi, c.SEQ_P)
        xp[a - lo:b - lo] = x_prompt[0, a:b]
        kb[a - lo:b - lo] = 0.0
        hval = np.zeros((128, 2), np.float32)
        hval[:, 0] = 1.0 if t0 - 1 >= 0 else 0.0
        hval[:, 1] = 1.0 if t0 + c.OWN < c.SEQ_P else 0.0
        m = dict(shared)
        m.update(xp=xp, kbp=kb, xs=np.ascontiguousarray(x_sample[core]), kbs=np.zeros((c.LK_S, 1), np.float32), hval=hval,
                 memp=np.ascontiguousarray(np.asarray(inputs["mem_prompt"], np.float32)[0]),
                 mems=np.ascontiguousarray(np.asarray(inputs["mem_sample"], np.float32)[core]))
        in_maps.append(m)
    res = run_bass_kernel_spmd(nc, in_maps, core_ids=list(range(8)))
    y_prompt = np.concatenate([res.results[i]["yp"] for i in range(8)], axis=0)[None]
    y_sample = np.stack([res.results[i]["ys"] for i in range(8)], axis=0)
    return (y_prompt.astype(np.float32), y_sample.astype(np.float32))


def kernel(**inputs):
    return run(Cfg(), inputs)
```
